# Optimizing a Trainium2 kernel written in Bass

```python
import jax, jax.numpy as jnp
from jax import lax
import numpy as np

D_MODEL = 1024
BATCH = 1
SEQ = 16384
DEPTH = 2
DEC_BATCH = 32
DEC_SEQ = 16
PAST_LEN = 1024

CHUNK = 64
N_EVEN = (DEPTH + 1) // 2
N_ODD = DEPTH // 2
HA = 8
DHA = 64
WA = HA * DHA
BAND_CHUNKS = 8
BAND_PAST = BAND_CHUNKS * CHUNK
REL_CLIP = 256
HB = 8
DHB = 64
WB = HB * DHB
QBLK = 128
FORGET_BIAS_INIT = 3.0
HG = 8
DKG = 128
DVG = 128
WG = HG * DKG
CONV = 4
MEM = 256
HX = 4
DHX = D_MODEL // HX
D_FF = 4 * D_MODEL
EPS = 1e-6
f32 = jnp.float32

kernel_name = 'streaming_hybrid_band_fox_gdn_step'


def rmsnorm(x, g):
    xf = x.astype(f32)
    y = xf * lax.rsqrt(jnp.mean(xf * xf, axis=-1, keepdims=True) + EPS)
    return (y * g.astype(f32)).astype(x.dtype)


def macaron_half(x, g, wg, wu, wd):
    h = rmsnorm(x, g)
    return x + 0.5 * ((jax.nn.silu(h @ wg) * (h @ wu)) @ wd)


def rel_bias(table, rel):
    return table[:, jnp.clip(rel, -REL_CLIP, REL_CLIP) + REL_CLIP].astype(f32)


def band_attn_prompt(q, k, v, table):
    b, s = q.shape[:2]
    n = s // CHUNK
    nk = (BAND_CHUNKS + 1) * CHUNK
    qc = q.reshape(b, n, CHUNK, HA, DHA)
    pad = jnp.zeros((b, BAND_PAST, HA, DHA), k.dtype)
    kc = jnp.concatenate([pad, k], axis=1).reshape(b, n + BAND_CHUNKS, CHUNK, HA, DHA)
    vc = jnp.concatenate([pad.astype(v.dtype), v], axis=1).reshape(b, n + BAND_CHUNKS, CHUNK, HA, DHA)
    kband = jnp.concatenate([kc[:, j:j + n] for j in range(BAND_CHUNKS + 1)], axis=2)
    vband = jnp.concatenate([vc[:, j:j + n] for j in range(BAND_CHUNKS + 1)], axis=2)
    qi = jnp.arange(CHUNK)
    ki = jnp.arange(nk)
    bias = rel_bias(table, qi[:, None] + BAND_PAST - ki[None, :])
    kpos = (jnp.arange(n)[:, None] - BAND_CHUNKS) * CHUNK + ki[None, :]
    valid = kpos >= 0
    sc = jnp.einsum('bnqhd,bnkhd->bnhqk', qc, kband).astype(f32) * (DHA ** -0.5) + bias
    sc = jnp.where(valid[None, :, None, None, :], sc, -jnp.inf)
    p = jax.nn.softmax(sc, axis=-1).astype(vband.dtype)
    o = jnp.einsum('bnhqk,bnkhd->bnqhd', p, vband)
    return o.reshape(b, s, HA, DHA)


def band_attn_sample(q, k, v, ck, cv, table):
    a = ck.shape[1]
    m = q.shape[1]
    K = jnp.concatenate([ck, k], axis=1)
    V = jnp.concatenate([cv, v], axis=1)
    kpos = jnp.concatenate([jnp.arange(a) - a, jnp.arange(m)])
    bias = rel_bias(table, jnp.arange(m)[:, None] - kpos[None, :])
    sc = jnp.einsum('bqhd,bkhd->bhqk', q, K).astype(f32) * (DHA ** -0.5) + bias
    p = jax.nn.softmax(sc, axis=-1).astype(V.dtype)
    return jnp.einsum('bhqk,bkhd->bqhd', p, V)


def fox_prompt(q, k, v, logf):
    b, s = q.shape[:2]
    nb = s // QBLK
    Ft = jnp.cumsum(logf, axis=1).transpose(0, 2, 1)
    qb = q.reshape(b, nb, QBLK, HB, DHB).transpose(1, 0, 2, 3, 4)
    Fq = Ft.reshape(b, HB, nb, QBLK).transpose(2, 0, 1, 3)
    kpos = jnp.arange(s)

    def block(args):
        qi, Fi, i0 = args
        qpos = i0 + jnp.arange(QBLK)
        sc = jnp.einsum('bqhd,bkhd->bhqk', qi, k).astype(f32) * (DHB ** -0.5)
        sc = sc + Fi[..., :, None] - Ft[..., None, :]
        sc = jnp.where(kpos[None, :] <= qpos[:, None], sc, -jnp.inf)
        p = jax.nn.softmax(sc, axis=-1).astype(v.dtype)
        return jnp.einsum('bhqk,bkhd->bqhd', p, v)

    o = lax.map(block, (qb, Fq, jnp.arange(nb) * QBLK))
    return o.transpose(1, 0, 2, 3, 4).reshape(b, s, HB, DHB)


def fox_sample(q, k, v, logf, ck, cv, clogf):
    P = ck.shape[1]
    m = q.shape[1]
    K = jnp.concatenate([ck, k], axis=1)
    V = jnp.concatenate([cv, v], axis=1)
    Ft = jnp.cumsum(jnp.concatenate([clogf.astype(f32), logf], axis=1), axis=1).transpose(0, 2, 1)
    sc = jnp.einsum('bqhd,bkhd->bhqk', q, K).astype(f32) * (DHB ** -0.5)
    sc = sc + Ft[..., P:, None] - Ft[..., None, :]
    causal = jnp.arange(P + m)[None, :] <= (P + jnp.arange(m))[:, None]
    sc = jnp.where(causal, sc, -jnp.inf)
    p = jax.nn.softmax(sc, axis=-1).astype(V.dtype)
    return jnp.einsum('bhqk,bkhd->bqhd', p, V)


def ab_split(h, w_in, b_f):
    b, L, _ = h.shape
    p = h @ w_in
    heads_a = lambda t: t.reshape(b, L, HA, DHA)
    heads_b = lambda t: t.reshape(b, L, HB, DHB)
    qa = heads_a(p[..., 0:WA])
    ka = heads_a(p[..., WA:2 * WA])
    va = heads_a(p[..., 2 * WA:3 * WA])
    o = 3 * WA
    qb = heads_b(p[..., o:o + WB])
    kb = heads_b(p[..., o + WB:o + 2 * WB])
    vb = heads_b(p[..., o + 2 * WB:o + 3 * WB])
    logf = jax.nn.log_sigmoid(p[..., o + 3 * WB:].astype(f32) + b_f.astype(f32))
    return qa, ka, va, qb, kb, vb, logf


def ab_prompt(h, w_in, b_f, table, w_o):
    b, L, _ = h.shape
    qa, ka, va, qb, kb, vb, logf = ab_split(h, w_in, b_f)
    oa = band_attn_prompt(qa, ka, va, table)
    ob = fox_prompt(qb, kb, vb, logf)
    y = jnp.concatenate([oa.reshape(b, L, WA), ob.reshape(b, L, WB)], axis=-1) @ w_o
    keep = min(BAND_PAST, L)
    return y, (ka[:, L - keep:], va[:, L - keep:], kb, vb, logf)


def ab_sample(h, w_in, b_f, table, w_o, ca_k, ca_v, cb_k, cb_v, cb_logf):
    b, L, _ = h.shape
    qa, ka, va, qb, kb, vb, logf = ab_split(h, w_in, b_f)
    oa = band_attn_sample(qa, ka, va, ca_k, ca_v, table)
    ob = fox_sample(qb, kb, vb, logf, cb_k, cb_v, cb_logf)
    y = jnp.concatenate([oa.reshape(b, L, WA), ob.reshape(b, L, WB)], axis=-1) @ w_o
    return y, (ka, va, kb, vb, logf)


def l2norm(x):
    return x * lax.rsqrt(jnp.sum(x * x, axis=-1, keepdims=True) + EPS)


def gated_delta_chunked(q, k, v, g, beta, S0, chunk):
    b, L, H, dk = q.shape
    dv = v.shape[-1]
    n = L // chunk
    C = chunk
    q = (q * (dk ** -0.5)).reshape(b, n, C, H, dk)
    k = k.reshape(b, n, C, H, dk)
    v = v.reshape(b, n, C, H, dv)
    beta = beta.reshape(b, n, C, H)
    gc = jnp.cumsum(g.reshape(b, n, C, H), axis=2)
    gct = gc.transpose(0, 1, 3, 2)
    tril = jnp.tril(jnp.ones((C, C), bool))
    strict = jnp.tril(jnp.ones((C, C), bool), -1)
    Lm = jnp.exp(jnp.where(tril, gct[..., :, None] - gct[..., None, :], -jnp.inf))
    kb = k * beta[..., None]
    M = jnp.where(strict, jnp.einsum('bnihd,bnjhd->bnhij', kb, k) * Lm, 0.0)
    eye = jnp.eye(C, dtype=f32)
    T = lax.linalg.triangular_solve(eye + M, jnp.broadcast_to(eye, M.shape), left_side=True, lower=True, unit_diagonal=True)
    u = jnp.einsum('bnhij,bnjhd->bnihd', T, v * beta[..., None])
    w = jnp.einsum('bnhij,bnjhd->bnihd', T, kb * jnp.exp(gc)[..., None])
    Aqk = jnp.where(tril, jnp.einsum('bnihd,bnjhd->bnhij', q, k) * Lm, 0.0)
    qg = q * jnp.exp(gc)[..., None]
    glast = gc[:, :, -1]
    kg = k * jnp.exp(glast[:, :, None, :] - gc)[..., None]
    xs = tuple(jnp.moveaxis(t, 1, 0) for t in (w, u, qg, kg, Aqk, glast))

    def step(S, inp):
        w_i, u_i, qg_i, kg_i, a_i, gl_i = inp
        vnew = u_i - jnp.einsum('bchk,bhkv->bchv', w_i, S)
        o = jnp.einsum('bchk,bhkv->bchv', qg_i, S) + jnp.einsum('bhij,bjhv->bihv', a_i, vnew)
        S = S * jnp.exp(gl_i)[..., None, None] + jnp.einsum('bchk,bchv->bhkv', kg_i, vnew)
        return S, o

    S, o = lax.scan(step, S0, xs)
    return jnp.moveaxis(o, 0, 1).reshape(b, L, H, dv), S


def gdn_mix(h, w_in, conv_w, a_log, dt_bias, norm_g, w_o, conv_state, S0, chunk):
    b, L, _ = h.shape
    p = h @ w_in
    qkv = p[..., :3 * WG]
    z = p[..., 3 * WG:4 * WG]
    a = p[..., 4 * WG:4 * WG + HG]
    bl = p[..., 4 * WG + HG:]
    xpad = jnp.concatenate([conv_state.astype(qkv.dtype), qkv], axis=1)
    conv = xpad[:, 0:L] * conv_w[0]
    for j in range(1, CONV):
        conv = conv + xpad[:, j:j + L] * conv_w[j]
    new_conv = xpad[:, xpad.shape[1] - (CONV - 1):]
    act = jax.nn.silu(conv.astype(f32))
    q = l2norm(act[..., :WG].reshape(b, L, HG, DKG))
    k = l2norm(act[..., WG:2 * WG].reshape(b, L, HG, DKG))
    v = act[..., 2 * WG:].reshape(b, L, HG, DVG)
    g = -jnp.exp(a_log.astype(f32)) * jax.nn.softplus(a.astype(f32) + dt_bias.astype(f32))
    beta = jax.nn.sigmoid(bl.astype(f32))
    o, S = gated_delta_chunked(q, k, v, g, beta, S0.astype(f32), chunk)
    o = rmsnorm(o, norm_g) * jax.nn.silu(z.reshape(b, L, HG, DVG).astype(f32))
    y = o.reshape(b, L, WG).astype(h.dtype) @ w_o
    return y, S, new_conv


def mem_kv(mem, g, wk, wv):
    b = mem.shape[0]
    m = rmsnorm(mem, g)
    return (m @ wk).reshape(b, MEM, HX, DHX), (m @ wv).reshape(b, MEM, HX, DHX)


def cross_attn(h, wq, wo, mk, mv):
    b, L, _ = h.shape
    q = (h @ wq).reshape(b, L, HX, DHX)
    sc = jnp.einsum('bqhd,bkhd->bhqk', q, mk).astype(f32) * (DHX ** -0.5)
    p = jax.nn.softmax(sc, axis=-1).astype(mv.dtype)
    o = jnp.einsum('bhqk,bkhd->bqhd', p, mv).reshape(b, L, HX * DHX)
    return o @ wo


def setup_inputs(seed: int = 0) -> dict:
    key = jax.random.key(seed)
    ks = iter(jax.random.split(key, 48))
    D = D_MODEL

    def nrm(shape, scale):
        return scale * jax.random.normal(next(ks), shape, f32)

    a_cache = min(BAND_PAST, PAST_LEN)
    dt = jnp.exp(jax.random.uniform(next(ks), (N_ODD, HG), f32, minval=float(np.log(1e-3)), maxval=float(np.log(1e-1))))
    return {
        'x_prompt': nrm((BATCH, SEQ, D), 1.0),
        'x_sample': nrm((DEC_BATCH, DEC_SEQ, D), 1.0),
        'mem_prompt': nrm((BATCH, MEM, D), 1.0),
        'cache_a_k': nrm((N_EVEN, DEC_BATCH, a_cache, HA, DHA), 1.0),
        'cache_a_v': nrm((N_EVEN, DEC_BATCH, a_cache, HA, DHA), 1.0),
        'cache_b_k': nrm((N_EVEN, DEC_BATCH, PAST_LEN, HB, DHB), 1.0),
        'cache_b_v': nrm((N_EVEN, DEC_BATCH, PAST_LEN, HB, DHB), 1.0),
        'cache_b_logf': jax.nn.log_sigmoid(FORGET_BIAS_INIT + nrm((N_EVEN, DEC_BATCH, PAST_LEN, HB), 1.0)),
        'state_gdn': nrm((N_ODD, DEC_BATCH, HG, DKG, DVG), 0.05),
        'state_gdn_conv': nrm((N_ODD, DEC_BATCH, CONV - 1, 3 * WG), 1.0),
        'cache_mem_k': nrm((DEPTH, DEC_BATCH, MEM, HX, DHX), 1.0),
        'cache_mem_v': nrm((DEPTH, DEC_BATCH, MEM, HX, DHX), 1.0),
        'norm_g': 1.0 + nrm((DEPTH, 4, D), 0.02),
        'mem_norm_g': 1.0 + nrm((DEPTH, D), 0.02),
        'final_norm_g': 1.0 + nrm((D,), 0.02),
        'ffn_w_gate': nrm((DEPTH, 2, D, D_FF), D ** -0.5),
        'ffn_w_up': nrm((DEPTH, 2, D, D_FF), D ** -0.5),
        'ffn_w_down': nrm((DEPTH, 2, D_FF, D), D_FF ** -0.5),
        'xa_w_q': nrm((DEPTH, D, HX * DHX), D ** -0.5),
        'xa_w_k': nrm((DEPTH, D, HX * DHX), D ** -0.5),
        'xa_w_v': nrm((DEPTH, D, HX * DHX), D ** -0.5),
        'xa_w_o': nrm((DEPTH, HX * DHX, D), (HX * DHX) ** -0.5),
        'ab_w_in': nrm((N_EVEN, D, 3 * WA + 3 * WB + HB), D ** -0.5),
        'ab_b_f': FORGET_BIAS_INIT + nrm((N_EVEN, HB), 0.5),
        'ab_rel_bias': nrm((N_EVEN, HA, 2 * REL_CLIP + 1), 0.5),
        'ab_w_o': nrm((N_EVEN, WA + WB, D), (WA + WB) ** -0.5),
        'gdn_w_in': nrm((N_ODD, D, 4 * WG + 2 * HG), D ** -0.5),
        'gdn_conv_w': nrm((N_ODD, CONV, 3 * WG), CONV ** -0.5),
        'gdn_a_log': jnp.log(jax.random.uniform(next(ks), (N_ODD, HG), f32, minval=1.0, maxval=16.0)),
        'gdn_dt_bias': dt + jnp.log(-jnp.expm1(-dt)),
        'gdn_norm_g': 1.0 + nrm((N_ODD, DVG), 0.02),
        'gdn_w_o': nrm((N_ODD, WG, D), WG ** -0.5),
    }


def reference(x_prompt, x_sample, mem_prompt, cache_a_k, cache_a_v, cache_b_k, cache_b_v, cache_b_logf,
              state_gdn, state_gdn_conv, cache_mem_k, cache_mem_v, norm_g, mem_norm_g, final_norm_g,
              ffn_w_gate, ffn_w_up, ffn_w_down, xa_w_q, xa_w_k, xa_w_v, xa_w_o,
              ab_w_in, ab_b_f, ab_rel_bias, ab_w_o,
              gdn_w_in, gdn_conv_w, gdn_a_log, gdn_dt_bias, gdn_norm_g, gdn_w_o):
    xp, xs = x_prompt, x_sample
    bp = xp.shape[0]
    akp, avp, bkp, bvp, blp, sgp, scp, mkp, mvp = [], [], [], [], [], [], [], [], []
    aks, avs, bks, bvs, bls, sgs, scs = [], [], [], [], [], [], []
    for l in range(DEPTH):
        xp = macaron_half(xp, norm_g[l, 0], ffn_w_gate[l, 0], ffn_w_up[l, 0], ffn_w_down[l, 0])
        xs = macaron_half(xs, norm_g[l, 0], ffn_w_gate[l, 0], ffn_w_up[l, 0], ffn_w_down[l, 0])
        hp = rmsnorm(xp, norm_g[l, 1])
        hs = rmsnorm(xs, norm_g[l, 1])
        if l % 2 == 0:
            e = l // 2
            yp, (ka, va, kb, vb, lf) = ab_prompt(hp, ab_w_in[e], ab_b_f[e], ab_rel_bias[e], ab_w_o[e])
            ys, (ka2, va2, kb2, vb2, lf2) = ab_sample(hs, ab_w_in[e], ab_b_f[e], ab_rel_bias[e], ab_w_o[e],
                                                      cache_a_k[e], cache_a_v[e], cache_b_k[e], cache_b_v[e], cache_b_logf[e])
            akp.append(ka); avp.append(va); bkp.append(kb); bvp.append(vb); blp.append(lf)
            aks.append(ka2); avs.append(va2); bks.append(kb2); bvs.append(vb2); bls.append(lf2)
        else:
            o = l // 2
            yp, Sp, cvp = gdn_mix(hp, gdn_w_in[o], gdn_conv_w[o], gdn_a_log[o], gdn_dt_bias[o], gdn_norm_g[o], gdn_w_o[o],
                                  jnp.zeros((bp, CONV - 1, 3 * WG), hp.dtype), jnp.zeros((bp, HG, DKG, DVG), f32), CHUNK)
            ys, Ss, cvs = gdn_mix(hs, gdn_w_in[o], gdn_conv_w[o], gdn_a_log[o], gdn_dt_bias[o], gdn_norm_g[o], gdn_w_o[o],
                                  state_gdn_conv[o], state_gdn[o], hs.shape[1])
            sgp.append(Sp); scp.append(cvp); sgs.append(Ss); scs.append(cvs)
        xp = xp + yp
        xs = xs + ys
        mk, mv = mem_kv(mem_prompt, mem_norm_g[l], xa_w_k[l], xa_w_v[l])
        mkp.append(mk); mvp.append(mv)
        xp = xp + cross_attn(rmsnorm(xp, norm_g[l, 2]), xa_w_q[l], xa_w_o[l], mk, mv)
        xs = xs + cross_attn(rmsnorm(xs, norm_g[l, 2]), xa_w_q[l], xa_w_o[l], cache_mem_k[l], cache_mem_v[l])
        xp = macaron_half(xp, norm_g[l, 3], ffn_w_gate[l, 1], ffn_w_up[l, 1], ffn_w_down[l, 1])
        xs = macaron_half(xs, norm_g[l, 3], ffn_w_gate[l, 1], ffn_w_up[l, 1], ffn_w_down[l, 1])
    y_prompt = rmsnorm(xp, final_norm_g)
    y_sample = rmsnorm(xs, final_norm_g)
    return (y_prompt, y_sample,
            jnp.stack(akp), jnp.stack(avp), jnp.stack(bkp), jnp.stack(bvp), jnp.stack(blp),
            jnp.stack(sgp), jnp.stack(scp), jnp.stack(mkp), jnp.stack(mvp),
            jnp.stack(aks), jnp.stack(avs), jnp.stack(bks), jnp.stack(bvs), jnp.stack(bls),
            jnp.stack(sgs), jnp.stack(scs))
```

```python
import contextlib
import numpy as np
import concourse.bass as bass
import concourse.mybir as mybir
from concourse.bass_utils import run_bass_kernel_spmd

F32 = mybir.dt.float32
BF16 = mybir.dt.bfloat16
AF = mybir.ActivationFunctionType
ALU = mybir.AluOpType
AX = mybir.AxisListType
ENGS = ("sp", "act", "dve", "pool", "pe")

NCORES = 8
D = 1024
KT = 8
TP = 2048
HALO = 512
TS = 64
DFF = 4096
FT = 32
EPS = 1e-6


class Buf:
    __slots__ = ("name", "lw", "rd", "const", "key")

    def __init__(self, name, key=None):
        self.name = name
        self.lw = None
        self.rd = []
        self.const = False
        self.key = key


class Op:
    __slots__ = ("eng", "fn", "deps", "sig", "sem", "val", "isdma", "nparts", "wr", "rdb", "inc", "slot")


class Sched:
    def __init__(self, nc):
        self.nc = nc
        self.streams = {e: [] for e in ENGS}
        self.stack = contextlib.ExitStack()
        self.allops = []
        self.n = 0
        self.slotmap = {}
        self.nslot = {e: 0 for e in ENGS}
        self.NSLOT = {"sp": 62, "pool": 24, "act": 4, "dve": 2, "pe": 2}
        self.lastdma = {}

    def sb(self, name, shape, dt):
        return self.stack.enter_context(self.nc.sbuf_tensor(name, list(shape), dt))

    def ps(self, name, shape, dt=F32):
        return self.stack.enter_context(self.nc.psum_tensor(name, list(shape), dt))

    def op(self, eng, fn, reads=(), writes=(), dma=False, nparts=1, inc=None):
        o = Op()
        o.eng = eng
        o.fn = fn
        o.isdma = dma
        o.nparts = nparts
        o.deps = {}
        o.sig = False
        o.sem = None
        o.val = 0
        o.inc = inc
        o.wr = tuple(writes)
        o.rdb = tuple(reads)
        for b in reads:
            if b.lw is not None:
                o.deps[id(b.lw)] = (b.lw, True)
        for b in writes:
            if b.lw is not None and id(b.lw) not in o.deps:
                o.deps[id(b.lw)] = (b.lw, False)
            for r in b.rd:
                if id(r) not in o.deps:
                    o.deps[id(r)] = (r, False)
        o.slot = None
        if dma:
            kb = o.wr[0] if o.wr else o.rdb[0]
            k = (kb.key if kb.key is not None else id(kb), eng)
            slot = self.slotmap.get(k)
            if slot is None:
                slot = (eng, self.nslot[eng] % self.NSLOT[eng])
                self.nslot[eng] += 1
                self.slotmap[k] = slot
            prev = self.lastdma.get(slot)
            if prev is not None and id(prev) not in o.deps:
                o.deps[id(prev)] = (prev, True)
            self.lastdma[slot] = o
            o.slot = slot
        for b in reads:
            if not b.const:
                b.rd.append(o)
        for b in writes:
            b.lw = o
            b.rd = []
        self.streams[eng].append(o)
        self.allops.append(o)
        self.n += 1
        return o

    def dma(self, q, out, in_, reads=(), writes=(), **kw):
        return self.op(q, lambda e: e.dma_start(out=out, in_=in_, **kw), reads, writes, dma=True)

    @staticmethod
    def _needed(o, d, raw):
        if d.isdma:
            return True
        if d.eng != o.eng:
            return True
        if o.isdma:
            return True
        if o.eng == "pe":
            return False
        return True

    def emit(self, final_wait_eng="sp"):
        nc = self.nc
        needed = self._needed
        for e in ENGS:
            for o in self.streams[e]:
                for d, raw in o.deps.values():
                    if needed(o, d, raw):
                        d.sig = True
        esem = {e: self.stack.enter_context(nc.semaphore("s_" + e)) for e in ENGS}
        keysem = {}
        keycnt = {}
        ecnt = {e: 0 for e in ENGS}
        self.keylog = {}
        for o in self.allops:
            if o.isdma:
                kb = o.wr[0] if o.wr else o.rdb[0]
                k = o.slot
                if k not in keysem:
                    keysem[k] = self.stack.enter_context(nc.semaphore("d_%d" % len(keysem)))
                    keycnt[k] = 0
                step = o.inc if o.inc is not None else 16
                keycnt[k] += step * o.nparts
                o.sem = keysem[k]
                o.val = keycnt[k]
                o.sig = True
                self.keylog.setdefault(list(keysem.keys()).index(k), []).append((o.eng, kb.name, str(k)[:40]))
            elif o.sig:
                ecnt[o.eng] += 1
                o.sem = esem[o.eng]
                o.val = ecnt[o.eng]
        self.maxvals = (dict(ecnt), max(keycnt.values()) if keycnt else 0, len(keysem))
        print('sched: ops', self.n, 'maxvals', self.maxvals, flush=True)
        self.nsem = len(keysem) + len(esem)
        final = [(keysem[k], keycnt[k]) for k in keysem]
        streams = self.streams

        def run(ename, eng):
            known = {}
            for o in streams[ename]:
                want = {}
                for d, raw in o.deps.values():
                    if not needed(o, d, raw):
                        continue
                    sid = id(d.sem)
                    if known.get(sid, 0) >= d.val:
                        continue
                    if sid not in want or want[sid][1] < d.val:
                        want[sid] = (d.sem, d.val)
                for sid, (sem_, val_) in want.items():
                    eng.wait_ge(sem_, val_)
                    known[sid] = val_
                r = o.fn(eng)
                if o.sig:
                    if o.isdma:
                        step = o.inc if o.inc is not None else 16
                        if isinstance(r, (list, tuple)):
                            for x in r:
                                x.then_inc(o.sem, step)
                        else:
                            r.then_inc(o.sem, step)
                    else:
                        r.then_inc(o.sem, 1)
            if ename == final_wait_eng:
                for s, v in final:
                    if known.get(id(s), 0) < v:
                        eng.wait_ge(s, v)

        with nc.Block() as block:
            @block.sync
            def _(e):
                run("sp", e)

            @block.scalar
            def _(e):
                run("act", e)

            @block.vector
            def _(e):
                run("dve", e)

            @block.gpsimd
            def _(e):
                run("pool", e)

            @block.tensor
            def _(e):
                run("pe", e)
        self.stack.close()


class Arena:
    def __init__(self, S, name, nelem, dt):
        self.t = S.sb(name, [128, nelem], dt)
        self.name = name
        self.n = nelem
        self.off = 0
        self.live = []
        self.carry = []

    def reset(self):
        seen = {}
        for b in self.live:
            if b.lw is not None:
                seen[id(b.lw)] = b.lw
            for r in b.rd:
                seen[id(r)] = r
        for o in self.carry:
            seen[id(o)] = o
        self.carry = list(seen.values())
        self.live = []
        self.off = 0

    def alloc(self, shape, name="a", inherit=None):
        n = 1
        for d_ in shape:
            n *= d_
        assert self.off + n <= self.n, (self.name, name, self.off, n, self.n)
        v = self.t[:, self.off:self.off + n]
        if len(shape) == 2:
            v = v.rearrange("p (a b) -> p a b", a=shape[0])
        elif len(shape) == 3:
            v = v.rearrange("p (a b c) -> p a b c", a=shape[0], b=shape[1])
        b = Buf(name, key=(self.name, self.off))
        b.rd = list(self.carry)
        if inherit is not None:
            b.lw = inherit.lw
        self.live.append(b)
        self.off += n
        return v, b


class Ring:
    def __init__(self, S, name, n, shape, dt, psum=False, arena=None):
        if arena is not None:
            tb = [arena.alloc(shape, "%s%d" % (name, i)) for i in range(n)]
            self.t = [x[0] for x in tb]
            self.b = [x[1] for x in tb]
        else:
            self.t = [(S.ps if psum else S.sb)("%s%d" % (name, i), [128] + list(shape), dt) for i in range(n)]
            self.b = [Buf("%s%d" % (name, i)) for i in range(n)]
        self.i = 0
        self.n = n

    def next(self):
        k = self.i % self.n
        self.i += 1
        return self.t[k], self.b[k]


NKA = HALO + TP + TS
NOWN = TP + TS
STAGE_END = 99
NOFFN = False
SKIP = set()
FOXQ = None


def build(stage_end=None):
    if stage_end is None:
        stage_end = STAGE_END
    nc = bass.Bass("TRN2", target_bir_lowering=False)
    S = Sched(nc)

    def din(name, shape, dt=F32):
        return nc.dram_tensor(name, list(shape), dt, kind="ExternalInput")

    def dout(name, shape, dt=F32):
        return nc.dram_tensor(name, list(shape), dt, kind="ExternalOutput")

    def dscr(name, shape, dt=BF16):
        return nc.dram_tensor(name, list(shape), dt)

    xp_in = din("xp", [HALO + TP, D])
    xs_in = din("xs", [TS, D])
    mem_in = din("mem", [256, D])
    norm_g = din("norm_g", [2, 4, D])
    mem_norm_g = din("mem_norm_g", [2, D])
    final_norm_g = din("final_norm_g", [D])
    ffn_wg = din("ffn_w_gate", [2, 2, D, DFF] if not NOFFN else [1, 1, 8, 8])
    ffn_wu = din("ffn_w_up", [2, 2, D, DFF] if not NOFFN else [1, 1, 8, 8])
    ffn_wd = din("ffn_w_down", [2, 2, DFF, D] if not NOFFN else [1, 1, 8, 8])
    xa_wq = din("xa_w_q", [2, D, D])
    xa_wk = din("xa_w_k", [2, D, D])
    xa_wv = din("xa_w_v", [2, D, D])
    xa_wo = din("xa_w_o", [2, D, D])
    ab_w_in = din("ab_w_in", [1, D, 3080])
    ab_b_f = din("ab_b_f", [1, 8])
    ab_rel = din("ab_rel_bias", [1, 8, 513])
    ab_w_o = din("ab_w_o", [1, D, D])
    gdn_w_in = din("gdn_w_in", [1, D, 4112])
    gdn_conv_w = din("gdn_conv_w", [1, 4, 3072])
    gdn_a_log = din("gdn_a_log", [1, 8])
    gdn_dt_bias = din("gdn_dt_bias", [1, 8])
    gdn_norm_g = din("gdn_norm_g", [1, 128])
    gdn_w_o = din("gdn_w_o", [1, D, D])
    st_gdn = din("st_gdn", [4, 8, 128, 128])
    st_conv = din("st_conv", [4, 3, 3072])
    meta = din("meta", [128, 32])
    c_ak = din("c_ak", [4, 512, 512])
    c_av = din("c_av", [4, 512, 512])
    c_bk = din("c_bk", [4, 1024, 512])
    c_bv = din("c_bv", [4, 1024, 512])
    c_bl = din("c_bl", [4, 1024, 8])
    c_mk = din("c_mk", [2, 4, 256, D])
    c_mv = din("c_mv", [2, 4, 256, D])

    o_yp = dout("o_yp", [TP, D])
    o_ys = dout("o_ys", [TS, D])
    o_akp = dout("o_akp", [512, 512])
    o_avp = dout("o_avp", [512, 512])
    o_bkp = dout("o_bkp", [TP, 512])
    o_bvp = dout("o_bvp", [TP, 512])
    o_blp = dout("o_blp", [TP, 8])
    o_mkp = dout("o_mkp", [2, 256, D])
    o_mvp = dout("o_mvp", [2, 256, D])
    o_aks = dout("o_aks", [TS, 512])
    o_avs = dout("o_avs", [TS, 512])
    o_bks = dout("o_bks", [TS, 512])
    o_bvs = dout("o_bvs", [TS, 512])
    o_bls = dout("o_bls", [TS, 8])
    o_dbg = dout("o_dbg", [KT, 128, NOWN], BF16)
    o_gcp = dout("o_gcp", [3, 3072])
    o_gcs = dout("o_gcs", [4, 3, 3072])
    o_gsp = dout("o_gsp", [8, 128, 128])
    o_gss = dout("o_gss", [4, 8, 128, 128])

    QA_d = dscr("QA_d", [4, 128, NOWN])
    QB_d = dscr("QB_d", [4, 128, NOWN])
    KA_d = dscr("KA_d", [4, 128, NKA])
    KB_d = dscr("KB_d", [4, 128, NOWN])
    VA_d = dscr("VA_d", [NKA, 512])
    VB_d = dscr("VB_d", [NOWN, 512])
    EXT_d = dscr("EXT_d", [8, 1024], F32)
    O_d = dscr("O_d", [KT, 128, NOWN])
    B_O = Buf("O_d")
    OL_d = dscr("OL_d", [8, 128, NOWN], F32)
    OP_d = dscr("OP_d", [8, 128, TP])
    Z_d = dscr("Z_d", [8, 128, NOWN], F32)
    G_d = dscr("G_d", [KT, 128, NOWN])
    XS = [dscr("XS%d" % i, [256, 256], F32) for i in range(4)]
    XR = [dscr("XR%d" % i, [8 * 256, 256], F32) for i in range(4)]
    GCs = dscr("GCs", [3, 3072], F32)
    GCr = dscr("GCr", [24, 3072], F32)
    B_OL, B_OP, B_Z, B_G, B_GCs, B_GCr = [Buf(n) for n in "OL OP Z G GCs GCr".split()]
    B_XS = [Buf("XS%d" % i) for i in range(4)]
    B_XR = [Buf("XR%d" % i) for i in range(4)]
    B_QA, B_QB, B_KA, B_KB, B_VA, B_VB, B_EXT = [Buf(n) for n in "QA QB KA KB VA VB EXT".split()]
    KBs = [dscr("KBs%d" % i, [128, 1024]) for i in range(8)]
    KBr = [dscr("KBr%d" % i, [8 * 128, 1024]) for i in range(8)]
    VBs = [dscr("VBs%d" % i, [256, 512]) for i in range(8)]
    VBr = [dscr("VBr%d" % i, [8 * 256, 512]) for i in range(8)]
    FCs = dscr("FCs", [128, 16 * 8 + 8], F32)
    FCr = dscr("FCr", [8 * 128, 16 * 8 + 8], F32)
    B_KBs = [Buf("KBs%d" % i) for i in range(8)]
    B_KBr = [Buf("KBr%d" % i) for i in range(8)]
    B_VBs = [Buf("VBs%d" % i) for i in range(8)]
    B_VBr = [Buf("VBr%d" % i) for i in range(8)]
    B_FCs, B_FCr = Buf("FCs"), Buf("FCr")

    xT = S.sb("xT", [128, KT, NOWN], F32)
    B_x = [Buf("x%d" % i) for i in range(5)]
    ident = S.sb("ident", [128, 128], F32)
    B_ident = Buf("ident")
    ones_f = S.sb("ones_f", [128, 128], F32)
    ones_b = S.sb("ones_b", [128, 128], BF16)
    triu_f = S.sb("triu_f", [128, 128], F32)
    triu_b = S.sb("triu_b", [128, 128], BF16)
    antiI = S.sb("antiI", [64, 64], F32)
    antiI128 = S.sb("antiI128", [128, 128], F32)
    ident_b = S.sb("ident_b", [128, 128], BF16)
    triu_s = S.sb("triu_s", [128, 128], F32)
    tril_s = S.sb("tril_s", [128, 128], F32)
    antiI16 = S.sb("antiI16", [16, 16], F32)
    B_ones = Buf("ones")
    gcol = S.sb("gcol", [128, 12, KT], F32)
    B_g = Buf("gcol")
    bf_bc = S.sb("bf_bc", [128, 8], F32)
    B_bf = Buf("bf_bc")
    meta_sb = S.sb("meta_sb", [128, 32], F32)
    B_meta = Buf("meta")
    hv_b = S.sb("hv_b", [128, 64], BF16)

    AB = Arena(S, "arena_b", 36864, BF16)
    AFa = Arena(S, "arena_f", 12288, F32)
    _pw = [S.ps("pw%d" % i, [128, 1024], F32) for i in range(4)]
    _pb = [Buf("bank%d" % i) for i in range(8)]

    class _PRing:
        def __init__(self, items):
            self.items = items
            self.i = 0

        def next(self):
            k = self.i % len(self.items)
            self.i += 1
            return self.items[k]

    PS = _PRing([(_pw[k // 2][:, (k % 2) * 512:(k % 2) * 512 + 512], _pb[k]) for k in range(6)])
    PA = _PRing([(_pw[3][:, 0:512], _pb[6]), (_pw[3][:, 512:1024], _pb[7])])
    PW = _PRing([(_pw[k], [_pb[2 * k], _pb[2 * k + 1]]) for k in range(3)])

    S.op("pool", lambda e: e.memset(ones_f[:], 1.0), [], [B_ones])
    S.op("pool", lambda e: e.memset(ones_b[:], 1.0), [], [B_ones])
    S.op("pool", lambda e: e.memset(ident[:], 1.0), [], [B_ident])
    S.op("pool", lambda e: e.affine_select(out=ident[:], in_=ident[:], compare_op=ALU.is_ge, fill=0.0, base=0,
                                           pattern=[[-1, 128]], channel_multiplier=1), [B_ident], [B_ident])
    S.op("pool", lambda e: e.affine_select(out=ident[:], in_=ident[:], compare_op=ALU.is_ge, fill=0.0, base=0,
                                           pattern=[[1, 128]], channel_multiplier=-1), [B_ident], [B_ident])
    S.op("pool", lambda e: e.memset(triu_f[:], 1.0), [], [B_ident])
    S.op("pool", lambda e: e.affine_select(out=triu_f[:], in_=triu_f[:], compare_op=ALU.is_ge, fill=0.0, base=0,
                                           pattern=[[1, 128]], channel_multiplier=-1), [B_ident], [B_ident])
    S.op("pool", lambda e: e.tensor_copy(triu_b[:], triu_f[:]), [B_ident], [B_ident])
    S.op("pool", lambda e: e.tensor_copy(ident_b[:], ident[:]), [B_ident], [B_ident])
    S.op("pool", lambda e: e.memset(triu_s[:], 1.0), [], [B_ident])
    S.op("pool", lambda e: e.affine_select(out=triu_s[:], in_=triu_s[:], compare_op=ALU.is_gt, fill=0.0, base=0,
                                           pattern=[[1, 128]], channel_multiplier=-1), [B_ident], [B_ident])
    S.op("pool", lambda e: e.memset(tril_s[:], 1.0), [], [B_ident])
    S.op("pool", lambda e: e.affine_select(out=tril_s[:], in_=tril_s[:], compare_op=ALU.is_gt, fill=0.0, base=0,
                                           pattern=[[-1, 128]], channel_multiplier=1), [B_ident], [B_ident])
    S.op("pool", lambda e: e.memset(antiI[:], 1.0), [], [B_ident])
    S.op("pool", lambda e: e.affine_select(out=antiI[:], in_=antiI[:], compare_op=ALU.is_ge, fill=0.0, base=-63,
                                           pattern=[[1, 64]], channel_multiplier=1), [B_ident], [B_ident])
    S.op("pool", lambda e: e.affine_select(out=antiI[:], in_=antiI[:], compare_op=ALU.is_ge, fill=0.0, base=63,
                                           pattern=[[-1, 64]], channel_multiplier=-1), [B_ident], [B_ident])
    for (tt, n_) in ((antiI128, 128), (antiI16, 16)):
        S.op("pool", lambda e, tt=tt: e.memset(tt[:], 1.0), [], [B_ident])
        S.op("pool", lambda e, tt=tt, n_=n_: e.affine_select(out=tt[:], in_=tt[:], compare_op=ALU.is_ge, fill=0.0,
                                                             base=-(n_ - 1), pattern=[[1, n_]], channel_multiplier=1),
             [B_ident], [B_ident])
        S.op("pool", lambda e, tt=tt, n_=n_: e.affine_select(out=tt[:], in_=tt[:], compare_op=ALU.is_ge, fill=0.0,
                                                             base=(n_ - 1), pattern=[[-1, n_]], channel_multiplier=-1),
             [B_ident], [B_ident])
    B_ident.const = True
    B_ones.const = True
    S.dma("sp", gcol[:, 0:8, :], norm_g.ap().rearrange("l f (kt p) -> p (l f) kt", p=128), writes=[B_g],
          allow_slow_non_contiguous=True)
    S.dma("sp", gcol[:, 8:10, :], mem_norm_g.ap().rearrange("l (kt p) -> p l kt", p=128), writes=[B_g],
          allow_slow_non_contiguous=True)
    S.dma("sp", gcol[:, 10:11, :], final_norm_g.ap().rearrange("(o kt p) -> p o kt", p=128, o=1), writes=[B_g],
          allow_slow_non_contiguous=True)
    S.dma("sp", bf_bc[:], ab_b_f.ap()[0:1, :].partition_broadcast(128), writes=[B_bf])
    S.dma("sp", meta_sb[:], meta.ap(), writes=[B_meta])
    S.op("dve", lambda e: e.tensor_copy(hv_b[:], meta_sb[:, 0:1].to_broadcast([128, 64])), [B_meta], [B_meta])

    R = {}

    def alloc_ffn():
        AB.reset()
        AFa.reset()
        R["hT"] = Ring(S, "hT", 2, [KT, 512], BF16, arena=AB)
        R["actT"] = Ring(S, "actT", 1, [FT, 512], BF16, arena=AB)
        R["wgu"] = Ring(S, "wgu", 2, [2, KT, 128], BF16, arena=AB)
        R["wdn"] = Ring(S, "wdn", 2, [FT, 128], BF16, arena=AB)
        R["sq"] = Ring(S, "sq", 1, [KT, 512], F32, arena=AFa)
        R["rstd"] = Ring(S, "rstd", 2, [512], F32, arena=AFa)
        R["tmpf"] = Ring(S, "tmpf", 2, [512], F32, arena=AFa)
        R["stage"] = Ring(S, "stage", 2, [1024], F32, arena=AFa)
        R["xin"] = Ring(S, "xin", 1, [D], F32, arena=AFa)

    def load_transposed(src_rows_ap, nrows, dstT, dcol, dbufs):
        t, tb = R["xin"].next()
        S.dma("sp", t[0:nrows, :], src_rows_ap, writes=[tb])
        for half in range(2):
            p, pb = PS.next()
            for j in range(4):
                kt = half * 4 + j
                S.op("pe", lambda e, p=p, t=t, kt=kt, j=j: e.transpose(
                    p[:, j * 128:j * 128 + nrows], t[0:nrows, kt * 128:(kt + 1) * 128], ident[0:nrows, 0:nrows]),
                    [tb, B_ident], [pb])
            S.op("dve", lambda e, p=p, half=half: e.tensor_copy(
                dstT[:, half * 4:half * 4 + 4, dcol:dcol + nrows],
                p[:].rearrange("p (j t) -> p j t", j=4)[:, :, 0:nrows]), [pb], dbufs)

    def store_transposed(srcT, scol, nrows, dst_rows_ap, sbufs):
        st, sb_ = R["stage"].next()
        for half in range(2):
            p, pb = PS.next()
            for j in range(4):
                kt = half * 4 + j
                S.op("pe", lambda e, p=p, kt=kt, j=j: e.transpose(
                    p[0:nrows, j * 128:(j + 1) * 128], srcT[:, kt, scol:scol + nrows], ident[:, :]),
                    list(sbufs) + [B_ident], [pb])
            S.op("act", lambda e, p=p, half=half: e.copy(st[0:nrows, half * 512:(half + 1) * 512], p[0:nrows, :]),
                 [pb], [sb_])
        S.dma("sp", dst_rows_ap, st[0:nrows, :], reads=[sb_])

    def rmsnorm(srcT, scol, T, sbufs, gidx):
        q, qb = R["sq"].next()
        W = q.shape[2]
        p, pb = PS.next()
        for w0 in range(0, T, W):
            w1 = min(T, w0 + W)
            S.op("act", lambda e, w0=w0, w1=w1: e.activation(q[:, :, 0:w1 - w0], srcT[:, :, scol + w0:scol + w1], AF.Square),
                 sbufs, [qb])
            for kt in range(KT):
                S.op("pe", lambda e, kt=kt, w0=w0, w1=w1: e.matmul(p[:, w0:w1], ones_f[:], q[:, kt, 0:w1 - w0], start=(kt == 0),
                                                                    stop=(kt == KT - 1)), [qb, B_ones], [pb])
        r, rb = R["rstd"].next()
        S.op("act", lambda e: e.activation(r[:, 0:T], p[:, 0:T], AF.Ln, bias=EPS, scale=1.0 / D), [pb], [rb])
        S.op("act", lambda e: e.activation(r[:, 0:T], r[:, 0:T], AF.Exp, scale=-0.5), [rb], [rb])
        h, hb = R["hT"].next()
        for kt in range(KT):
            S.op("dve", lambda e, kt=kt: e.scalar_tensor_tensor(
                out=h[:, kt, 0:T], in0=srcT[:, kt, scol:scol + T], scalar=gcol[:, gidx, kt:kt + 1],
                in1=r[:, 0:T], op0=ALU.mult, op1=ALU.mult), list(sbufs) + [rb, B_g], [hb])
        return h, hb

    def load_gu(l, f, c):
        w, wb = R["wgu"].next()
        S.dma("pool", w[:, 0, :, :], ffn_wg.ap()[l, f, :, c * 128:(c + 1) * 128].rearrange("(kt p) m -> p kt m", p=128),
              writes=[wb])
        S.dma("pool", w[:, 1, :, :], ffn_wu.ap()[l, f, :, c * 128:(c + 1) * 128].rearrange("(kt p) m -> p kt m", p=128),
              writes=[wb])
        return w, wb

    def load_dn(l, f, o):
        w, wb = R["wdn"].next()
        S.dma("pool", w[:], ffn_wd.ap()[l, f, :, o * 128:(o + 1) * 128].rearrange("(ft p) m -> p ft m", p=128),
              writes=[wb])
        return w, wb

    def ffn(l, f, gidx, dstT, dcol, T, dbufs):
        if NOFFN:
            return
        h, hb = rmsnorm(dstT, dcol, T, dbufs, gidx)
        a, ab = R["actT"].next()
        nxt = load_gu(l, f, 0)
        for c in range(FT):
            w, wb = nxt
            if c + 1 < FT:
                nxt = load_gu(l, f, c + 1)
            pg, pgb = PS.next()
            pu, pub = PS.next()
            for kt in range(KT):
                S.op("pe", lambda e, kt=kt, w=w, pg=pg: e.matmul(pg[:, 0:T], w[:, 0, kt, :], h[:, kt, 0:T],
                                                                    start=(kt == 0), stop=(kt == KT - 1)), [wb, hb], [pgb])
            for kt in range(KT):
                S.op("pe", lambda e, kt=kt, w=w, pu=pu: e.matmul(pu[:, 0:T], w[:, 1, kt, :], h[:, kt, 0:T],
                                                                    start=(kt == 0), stop=(kt == KT - 1)), [wb, hb], [pub])
            t, tb = R["tmpf"].next()
            S.op("act", lambda e, t=t, pg=pg: e.activation(t[:, 0:T], pg[:, 0:T], AF.Silu), [pgb], [tb])
            S.op("dve", lambda e, t=t, pu=pu, c=c: e.tensor_tensor(a[:, c, 0:T], t[:, 0:T], pu[:, 0:T], ALU.mult),
                 [tb, pub], [ab])
        nxt = load_dn(l, f, 0)
        for o in range(KT):
            w, wb = nxt
            if o + 1 < KT:
                nxt = load_dn(l, f, o + 1)
            py, pyb = PS.next()
            for ft in range(FT):
                S.op("pe", lambda e, ft=ft, w=w, py=py: e.matmul(py[:, 0:T], w[:, ft, :], a[:, ft, 0:T],
                                                                    start=(ft == 0), stop=(ft == FT - 1)), [wb, ab], [pyb])
            S.op("dve", lambda e, o=o, py=py: e.scalar_tensor_tensor(
                out=dstT[:, o, dcol:dcol + T], in0=py[:, 0:T], scalar=0.5, in1=dstT[:, o, dcol:dcol + T],
                op0=ALU.mult, op1=ALU.add), [pyb] + list(dbufs), dbufs)

    def load_w512(w_ap, col0, ncols, ring="wdn"):
        w, wb = R[ring].next()
        wv = w[:].rearrange("p (kt m) c -> p kt (m c)", kt=KT)
        S.dma("pool", wv[:, :, 0:ncols], w_ap[:, col0:col0 + ncols].rearrange("(kt p) m -> p kt m", p=128),
              writes=[wb])
        return wv, wb

    def load_w128(w_ap, col0):
        w, wb = R["wgu"].next()
        wv = w[:, 0, :, :]
        S.dma("pool", wv, w_ap[:, col0:col0 + 128].rearrange("(kt p) m -> p kt m", p=128), writes=[wb])
        return wv, wb

    def tokmajor(h, hb, r0, nrows, wv, wb, ncols):
        p, pb = PS.next()
        for kt in range(KT):
            S.op("pe", lambda e, kt=kt, p=p: e.matmul(p[0:nrows, 0:ncols], h[:, kt, r0:r0 + nrows], wv[:, kt, 0:ncols],
                                                     start=(kt == 0), stop=(kt == KT - 1)), [hb, wb], [pb])
        return p, pb

    def featmajor(h, hb, T, wv, wb):
        p, pb = PS.next()
        for kt in range(KT):
            S.op("pe", lambda e, kt=kt, p=p: e.matmul(p[:, 0:T], wv[:, kt, :], h[:, kt, 0:T],
                                                     start=(kt == 0), stop=(kt == KT - 1)), [hb, wb], [pb])
        return p, pb

    cnt = [0]

    def evac(dst_ap, src_ap, reads, writes):
        cnt[0] += 1
        if cnt[0] % 2:
            S.op("act", lambda e: e.copy(dst_ap, src_ap), reads, writes)
        else:
            S.op("dve", lambda e: e.tensor_copy(dst_ap, src_ap), reads, writes)

    alloc_ffn()
    xhT, B_xh = AFa.alloc([KT, 256], "xhT")
    ext_sb, B_exts = AFa.alloc([1024], "ext_sb")
    S.dma("sp", ext_sb[0:8, 0:513], ab_rel.ap()[0], writes=[B_exts])
    S.op("dve", lambda e: e.tensor_copy(ext_sb[0:8, 513:1024], ext_sb[0:8, 512:513].to_broadcast([8, 511])),
         [B_exts], [B_exts])
    S.dma("sp", EXT_d.ap()[:, :], ext_sb[0:8, :], reads=[B_exts], writes=[B_EXT])
    for i in range(TP // 128):
        load_transposed(xp_in.ap()[HALO + i * 128:HALO + (i + 1) * 128, :], 128, xT, i * 128, [B_x[i // 4]])
    load_transposed(xs_in.ap()[0:TS, :], TS, xT, TP, [B_x[4]])
    TILES = [(i * 512, 512, B_x[i]) for i in range(4)] + [(TP, TS, B_x[4])]

    win = ab_w_in.ap()[0]

    def project_l0(srcT, c0, T, sbufs, kind, tok0):
        h, hb = rmsnorm(srcT, c0, T, sbufs, 1)
        fm = [(512, KA_d, B_KA, HALO + tok0)] if kind != "halo" else [(512, KA_d, B_KA, tok0)]
        if kind != "halo":
            fm += [(0, QA_d, B_QA, tok0), (1536, QB_d, B_QB, tok0), (2048, KB_d, B_KB, tok0)]
        for (col0, dst, db, dc) in (fm if 'fm' not in SKIP else []):
            for pr in range(4):
                wv, wb = load_w128(win, col0 + pr * 128)
                p, pb = featmajor(h, hb, T, wv, wb)
                st, sb_ = R["obf"].next()
                evac(st[:, 0:T], p[:, 0:T], [pb], [sb_])
                S.dma("sp", dst.ap()[pr, :, dc:dc + T], st[:, 0:T], reads=[sb_], writes=[db])
        groups = [(1024, 512, "av")]
        if kind != "halo":
            groups += [(512, 512, "ak"), (2048, 512, "bk"), (2560, 512, "bv"), (3072, 8, "bl")]
        for (col0, ncols, nm) in groups:
            if kind == "own" and nm == "ak" and tok0 != 1536:
                continue
            wv, wb = load_w512(win, col0, ncols)
            for j in range((T + 127) // 128):
                nrows = min(128, T - j * 128)
                p, pb = tokmajor(h, hb, j * 128, nrows, wv, wb, ncols)
                r0 = tok0 + j * 128
                st, sb_ = R["stage"].next()
                if nm != "bl":
                    if kind == "halo":
                        S.op("dve", lambda e, st=st, p=p, nrows=nrows: e.tensor_scalar(
                            st[0:nrows, 0:512], p[0:nrows, 0:512], meta_sb[0:nrows, 0:1], None, ALU.mult),
                            [pb, B_meta], [sb_])
                    else:
                        evac(st[0:nrows, 0:ncols], p[0:nrows, 0:ncols], [pb], [sb_])
                    if nm in ("av", "bv") and 'tmbf' not in SKIP:
                        if kind == "halo":
                            S.dma("pool", VA_d.ap()[r0:r0 + nrows, :], st[0:nrows, 0:512], reads=[sb_], writes=[B_VA])
                        elif nm == "av":
                            S.dma("pool", VA_d.ap()[HALO + r0:HALO + r0 + nrows, :], st[0:nrows, 0:512],
                                  reads=[sb_], writes=[B_VA])
                        else:
                            S.dma("pool", VB_d.ap()[r0:r0 + nrows, :], st[0:nrows, 0:512], reads=[sb_],
                                  writes=[B_VB])
                if kind == "halo":
                    continue
                if kind == "own":
                    dst = {"ak": o_akp, "av": o_avp, "bk": o_bkp, "bv": o_bvp, "bl": o_blp}[nm]
                    ro = r0
                    if nm in ("ak", "av"):
                        if tok0 != 1536:
                            continue
                        ro = j * 128
                else:
                    dst = {"ak": o_aks, "av": o_avs, "bk": o_bks, "bv": o_bvs, "bl": o_bls}[nm]
                    ro = 0
                if nm != "bl":
                    S.dma("sp", dst.ap()[ro:ro + nrows, :], st[0:nrows, 0:ncols], reads=[sb_])
                else:
                    S.op("dve", lambda e, st=st, p=p, nrows=nrows: e.tensor_tensor(
                        st[0:nrows, 0:8], p[0:nrows, 0:8], bf_bc[0:nrows, :], ALU.add), [pb, B_bf], [sb_])
                    S.op("act", lambda e, st=st, nrows=nrows: e.activation(st[0:nrows, 0:8], st[0:nrows, 0:8], AF.Exp,
                                                                           scale=-1.0), [sb_], [sb_])
                    S.op("act", lambda e, st=st, nrows=nrows: e.activation(st[0:nrows, 0:8], st[0:nrows, 0:8], AF.Ln,
                                                                           bias=1.0), [sb_], [sb_])
                    S.op("dve", lambda e, st=st, nrows=nrows: e.tensor_scalar(
                        st[0:nrows, 8:16], st[0:nrows, 0:8], -1.0, None, ALU.mult), [sb_], [sb_])
                    S.dma("sp", dst.ap()[ro:ro + nrows, :], st[0:nrows, 8:16], reads=[sb_])
                    S.op("dve", lambda e, st=st, nrows=nrows, r0=r0: e.tensor_copy(
                        LF[0:nrows, r0 // 128, :], st[0:nrows, 8:16]), [sb_], [B_LF])

    LF = S.sb("LF", [128, 17, 8], F32)
    B_LF = Buf("LF")


    class _ObfRing:
        def __init__(self):
            self.i = 0

        def next(self):
            a, ab = R["actT"].t[0], R["actT"].b[0]
            k = self.i % 8
            self.i += 1
            return a[:, k * 4, :], ab

    R["obf"] = _ObfRing()

    for hh in range(2):
        for i in range(2):
            load_transposed(xp_in.ap()[hh * 256 + i * 128: hh * 256 + (i + 1) * 128, :], 128, xhT, i * 128, [B_xh])
        ffn(0, 0, 0, xhT, 0, 256, [B_xh])
        project_l0(xhT, 0, 256, [B_xh], "halo", hh * 256)
    for ti, (c0, T, xb) in enumerate(TILES):
        ffn(0, 0, 0, xT, c0, T, [xb])
        project_l0(xT, c0, T, [xb], "own" if ti < 4 else "sample", c0)

    def mem_outputs():
        memT, B_mem = R["sq"].t[0], R["sq"].b[0]
        for i in range(2):
            load_transposed(mem_in.ap()[i * 128:(i + 1) * 128, :], 128, memT, i * 128, [B_mem])
        for l in range(2):
            st, sb_ = R["stage"].next()
            p, pb = PS.next()
            for kt in range(KT):
                S.op("act", lambda e, kt=kt, st=st: e.activation(st[:, (kt % 2) * 256:(kt % 2) * 256 + 256],
                                                                  memT[:, kt, 0:256], AF.Square), [B_mem], [sb_])
                S.op("pe", lambda e, kt=kt, st=st, p=p: e.matmul(p[:, 0:256], ones_f[:],
                                                                    st[:, (kt % 2) * 256:(kt % 2) * 256 + 256],
                                                                    start=(kt == 0), stop=(kt == KT - 1)),
                     [sb_, B_ones], [pb])
            r, rb = R["rstd"].next()
            S.op("act", lambda e, r=r, p=p: e.activation(r[:, 0:256], p[:, 0:256], AF.Ln, bias=EPS, scale=1.0 / D),
                 [pb], [rb])
            S.op("act", lambda e, r=r: e.activation(r[:, 0:256], r[:, 0:256], AF.Exp, scale=-0.5), [rb], [rb])
            h, hb = R["hT"].next()
            for kt in range(KT):
                S.op("dve", lambda e, kt=kt, h=h, r=r, l=l: e.scalar_tensor_tensor(
                    out=h[:, kt, 0:256], in0=memT[:, kt, 0:256], scalar=gcol[:, 8 + l, kt:kt + 1],
                    in1=r[:, 0:256], op0=ALU.mult, op1=ALU.mult), [B_mem, rb, B_g], [hb])
            for (wsrc, dst) in ((xa_wk, o_mkp), (xa_wv, o_mvp)):
                for g in range(2):
                    wv, wb = load_w512(wsrc.ap()[l], g * 512, 512)
                    for j in range(2):
                        p2, pb2 = tokmajor(h, hb, j * 128, 128, wv, wb, 512)
                        st2, sb2 = R["stage"].next()
                        evac(st2[:, 0:512], p2[:, 0:512], [pb2], [sb2])
                        S.dma("sp", dst.ap()[l, j * 128:(j + 1) * 128, g * 512:(g + 1) * 512], st2[:, 0:512],
                              reads=[sb2])

    if 'mem' not in SKIP:
        mem_outputs()

    if stage_end <= 0:
        for i in range(TP // 128):
            store_transposed(xT, i * 128, 128, o_yp.ap()[i * 128:(i + 1) * 128, :], [B_x[i // 4]])
        store_transposed(xT, TP, TS, o_ys.ap()[0:TS, :], [B_x[4]])
        S.emit()
        return nc

    AB.reset()
    AFa.reset()
    qa_r = Ring(S, "qa", 2, [512], BF16, arena=AB)
    ka_r = Ring(S, "ka", 2, [1024], BF16, arena=AB)
    va_r = Ring(S, "va", 2, [16, 512], BF16, arena=AB)
    pt_r = Ring(S, "pt", 3, [576], BF16, arena=AB)
    oT_r = Ring(S, "oT", 1, [KT, 512], BF16, arena=AB)
    RBall, B_RB = AFa.alloc([8, 576], "RBall")
    rbx_r = Ring(S, "rbx", 2, [576], F32, arena=AFa)
    for h in range(8):
        rbx, rbxb = rbx_r.next()
        src = bass.AP(tensor=EXT_d, offset=h * 1024 + 193, ap=[[1, 64], [64, 9], [1, 64]])
        S.dma("sp", rbx[0:64, :].rearrange("p (a b) -> p a b", a=9), src, reads=[B_EXT], writes=[rbxb])
        pw_, pwb = PW.next()
        S.op("pe", lambda e, pw_=pw_, rbx=rbx: e.matmul(pw_[0:64, 0:512], antiI[:, :], rbx[0:64, 0:512],
                                                        start=True, stop=True), [rbxb, B_ident], pwb)
        S.op("pe", lambda e, pw_=pw_, rbx=rbx: e.matmul(pw_[0:64, 512:576], antiI[:, :], rbx[0:64, 512:576],
                                                        start=True, stop=True), [rbxb, B_ident], pwb)
        S.op("dve", lambda e, pw_=pw_, h=h: e.tensor_copy(RBall[0:64, h, :], pw_[0:64, 0:576]), pwb, [B_RB])
    sf_r = Ring(S, "sf", 2, [576], F32, arena=AFa)
    rc_r = Ring(S, "rc", 2, [512], F32, arena=AFa)

    def band_prompt_tile(ti):
        c0 = ti * 512
        oT, ob = oT_r.t[0], oT_r.b[0]
        va, vab = va_r.next()
        S.dma("sp", va[0:64, :, :], VA_d.ap()[c0:c0 + 1024, :].rearrange("(ch k) c -> k ch c", k=64),
              reads=[B_VA], writes=[vab])
        for pr in range(4):
            qa, qab = qa_r.next()
            ka, kab = ka_r.next()
            S.dma("sp", qa[:], QA_d.ap()[pr, :, c0:c0 + 512], reads=[B_QA], writes=[qab])
            S.dma("sp", ka[:], KA_d.ap()[pr, :, c0:c0 + 1024], reads=[B_KA], writes=[kab])
            for hh in range(2):
                h = pr * 2 + hh
                r0 = hh * 64
                rb, rbb = RBall[:, h, :], B_RB
                po, pob = PA.next()
                pd, pdb = PA.next()
                for j in range(8):
                    ps_, psb = PW.next()
                    for dl in range(9):
                        dk = 8 - dl
                        S.op("pe", lambda e, ps_=ps_, dk=dk, dl=dl, j=j, ka=ka, qa=qa, r0=r0: e.matmul(
                            ps_[0:64, dl * 64:(dl + 1) * 64], ka[r0:r0 + 64, (j + dk) * 64:(j + dk + 1) * 64],
                            qa[r0:r0 + 64, j * 64:(j + 1) * 64], start=True, stop=True), [kab, qab], psb)
                    sf, sfb = sf_r.next()
                    pt, ptb = pt_r.next()
                    S.op("dve", lambda e, sf=sf, ps_=ps_, rb=rb: e.scalar_tensor_tensor(
                        out=sf[0:64, :], in0=ps_[0:64, 0:576], scalar=0.125, in1=rb[0:64, :],
                        op0=ALU.mult, op1=ALU.add), psb + [rbb], [sfb])
                    S.op("act", lambda e, sf=sf, pt=pt: e.activation(pt[0:64, :], sf[0:64, :], AF.Exp), [sfb], [ptb])
                    for dl in range(9):
                        dk = 8 - dl
                        halo_key = (ti == 0 and j + dk < 8)
                        S.op("pe", lambda e, po=po, dk=dk, dl=dl, j=j, va=va, pt=pt, h=h: e.matmul(
                            po[0:64, j * 64:(j + 1) * 64], va[0:64, j + dk, h * 64:(h + 1) * 64],
                            pt[0:64, dl * 64:(dl + 1) * 64], start=(dl == 0), stop=(dl == 8)), [vab, ptb], [pob])
                        S.op("pe", lambda e, pd=pd, dl=dl, j=j, pt=pt, halo_key=halo_key: e.matmul(
                            pd[0:64, j * 64:(j + 1) * 64], (hv_b if halo_key else ones_b)[0:64, 0:64],
                            pt[0:64, dl * 64:(dl + 1) * 64], start=(dl == 0), stop=(dl == 8)),
                            [ptb, B_ones, B_meta], [pdb])
                rc, rcb = rc_r.next()
                S.op("dve", lambda e, rc=rc, pd=pd: e.reciprocal(rc[0:64, :], pd[0:64, :]), [pdb], [rcb])
                S.op("dve", lambda e, rc=rc, po=po, pr=pr, r0=r0: e.tensor_tensor(
                    oT[r0:r0 + 64, pr, :], po[0:64, :], rc[0:64, :], ALU.mult), [pob, rcb], [ob])

    for ti in range(4 if 'band' not in SKIP else 0):
        band_prompt_tile(ti)
        oT, ob = oT_r.t[0], oT_r.b[0]
        for kt in range(4):
            S.dma("sp", O_d.ap()[kt, :, ti * 512:(ti + 1) * 512], oT[:, kt, :], reads=[ob], writes=[B_O])

    oTs, B_oTs = AB.alloc([KT, TS], "oTs")
    SBc, B_SBc = AFa.alloc([8, 64], "SBc")
    SBn, B_SBn = AFa.alloc([8, 16], "SBn")
    for h in range(8):
        rbx, rbxb = rbx_r.next()
        src = bass.AP(tensor=EXT_d, offset=h * 1024 + 257, ap=[[1, 128], [128, 4], [1, 16]])
        S.dma("sp", rbx[:, 0:64].rearrange("p (a b) -> p a b", a=4), src, reads=[B_EXT], writes=[rbxb])
        src2 = bass.AP(tensor=EXT_d, offset=h * 1024 + 241, ap=[[1, 16], [1, 16]])
        S.dma("sp", rbx[0:16, 64:80], src2, reads=[B_EXT], writes=[rbxb])
        p_, pb_ = PS.next()
        S.op("pe", lambda e, p_=p_, rbx=rbx: e.matmul(p_[:, 0:64], antiI128[:, :], rbx[:, 0:64], start=True, stop=True),
             [rbxb, B_ident], [pb_])
        S.op("dve", lambda e, p_=p_, h=h: e.tensor_copy(SBc[:, h, :], p_[:, 0:64]), [pb_], [B_SBc])
        p2_, pb2_ = PS.next()
        S.op("pe", lambda e, p2_=p2_, rbx=rbx: e.matmul(p2_[0:16, 0:16], antiI16[:, :], rbx[0:16, 64:80], start=True,
                                                        stop=True), [rbxb, B_ident], [pb2_])
        S.op("dve", lambda e, p2_=p2_, h=h: e.tensor_copy(SBn[0:16, h, :], p2_[0:16, 0:16]), [pb2_], [B_SBn])

    ckf_r = Ring(S, "ckf", 1, [4, 512], F32, arena=AFa)
    kcT_r = Ring(S, "kcT", 1, [4, 512], BF16, arena=AB)
    cv_r = Ring(S, "cv", 1, [4, 512], BF16, arena=AB)
    qs_r = Ring(S, "qs", 1, [4, 16], BF16, arena=AB)
    kn_r = Ring(S, "kn", 1, [4, 16], BF16, arena=AB)
    vn_r = Ring(S, "vn", 1, [512], BF16, arena=AB)
    ps_r = Ring(S, "pts", 2, [80], BF16, arena=AB)
    ss_r = Ring(S, "ss", 2, [80], F32, arena=AFa)

    def band_sample(b):
        ckf, ckfb = ckf_r.next()
        S.dma("sp", ckf[:], c_ak.ap()[b].rearrange("(t p) c -> p t c", p=128), writes=[ckfb])
        cv, cvb = cv_r.next()
        S.dma("pool", cv[:], c_av.ap()[b].rearrange("(t p) c -> p t c", p=128), writes=[cvb])
        qs, qsb = qs_r.next()
        kn, knb = kn_r.next()
        vn, vnb = vn_r.next()
        for pr in range(4):
            S.dma("sp", qs[:, pr, :], QA_d.ap()[pr, :, TP + b * 16:TP + (b + 1) * 16], reads=[B_QA], writes=[qsb])
            S.dma("sp", kn[:, pr, :], KA_d.ap()[pr, :, HALO + TP + b * 16:HALO + TP + (b + 1) * 16], reads=[B_KA],
                  writes=[knb])
        S.dma("sp", vn[0:16, :], VA_d.ap()[HALO + TP + b * 16:HALO + TP + (b + 1) * 16, :], reads=[B_VA], writes=[vnb])
        kcT, kcTb = kcT_r.next()
        for pr in range(4):
            p_, pb_ = PS.next()
            for t in range(4):
                S.op("pe", lambda e, p_=p_, t=t, pr=pr, ckf=ckf: e.transpose(
                    p_[:, t * 128:(t + 1) * 128], ckf[:, t, pr * 128:(pr + 1) * 128], ident[:, :]), [ckfb, B_ident], [pb_])
            evac(kcT[:, pr, :], p_[:, 0:512], [pb_], [kcTb])
        for h in range(8):
            pr, r0 = h // 2, (h % 2) * 64
            p_, pb_ = PS.next()
            for tp in range(4):
                t = 3 - tp
                S.op("pe", lambda e, p_=p_, tp=tp, t=t, pr=pr, r0=r0: e.matmul(
                    p_[:, tp * 16:(tp + 1) * 16], kcT[r0:r0 + 64, pr, t * 128:(t + 1) * 128], qs[r0:r0 + 64, pr, :],
                    start=True, stop=True), [kcTb, qsb], [pb_])
            S.op("pe", lambda e, p_=p_, pr=pr, r0=r0: e.matmul(
                p_[0:16, 64:80], kn[r0:r0 + 64, pr, :], qs[r0:r0 + 64, pr, :], start=True, stop=True), [knb, qsb], [pb_])
            ss, ssb = ss_r.next()
            pt, ptb = ps_r.next()
            S.op("dve", lambda e, ss=ss, p_=p_, h=h: e.scalar_tensor_tensor(
                out=ss[:, 0:64], in0=p_[:, 0:64], scalar=0.125, in1=SBc[:, h, :], op0=ALU.mult, op1=ALU.add),
                [pb_, B_SBc], [ssb])
            S.op("dve", lambda e, ss=ss, p_=p_, h=h: e.scalar_tensor_tensor(
                out=ss[0:16, 64:80], in0=p_[0:16, 64:80], scalar=0.125, in1=SBn[0:16, h, :], op0=ALU.mult,
                op1=ALU.add), [pb_, B_SBn, ssb], [ssb])
            S.op("act", lambda e, ss=ss, pt=pt: e.activation(pt[:, 0:64], ss[:, 0:64], AF.Exp), [ssb], [ptb])
            S.op("act", lambda e, ss=ss, pt=pt: e.activation(pt[0:16, 64:80], ss[0:16, 64:80], AF.Exp), [ssb, ptb], [ptb])
            po, pob = PA.next()
            pd, pdb = PA.next()
            for tp in range(4):
                t = 3 - tp
                S.op("pe", lambda e, po=po, tp=tp, t=t, h=h, pt=pt: e.matmul(
                    po[0:64, 0:16], cv[:, t, h * 64:(h + 1) * 64], pt[:, tp * 16:(tp + 1) * 16],
                    start=(tp == 0), stop=False), [cvb, ptb], [pob])
                S.op("pe", lambda e, pd=pd, tp=tp, pt=pt: e.matmul(
                    pd[0:64, 0:16], ones_b[:, 0:64], pt[:, tp * 16:(tp + 1) * 16], start=(tp == 0), stop=False),
                    [ptb, B_ones], [pdb])
            S.op("pe", lambda e, po=po, h=h, pt=pt: e.matmul(
                po[0:64, 0:16], vn[0:16, h * 64:(h + 1) * 64], pt[0:16, 64:80], start=False, stop=True), [vnb, ptb], [pob])
            S.op("pe", lambda e, pd=pd, pt=pt: e.matmul(
                pd[0:64, 0:16], ones_b[0:16, 0:64], pt[0:16, 64:80], start=False, stop=True), [ptb, B_ones], [pdb])
            rc, rcb = rc_r.next()
            S.op("dve", lambda e, rc=rc, pd=pd: e.reciprocal(rc[0:64, 0:16], pd[0:64, 0:16]), [pdb], [rcb])
            S.op("dve", lambda e, rc=rc, po=po, pr=pr, r0=r0, b=b: e.tensor_tensor(
                oTs[r0:r0 + 64, pr, b * 16:(b + 1) * 16], po[0:64, 0:16], rc[0:64, 0:16], ALU.mult), [pob, rcb], [B_oTs])

    for b in range(4 if 'band' not in SKIP else 0):
        band_sample(b)
    for kt in range(4):
        S.dma("sp", O_d.ap()[kt, :, TP:TP + TS], oTs[:, kt, :], reads=[B_oTs], writes=[B_O])

    AB.reset()
    AFa.reset()
    Fc, B_Fc = AFa.alloc([16, 8], "Fc")
    Cend, B_Ce = AFa.alloc([16, 8], "Cend")
    fsa, B_fsa = AFa.alloc([16, 8], "fsa")
    fsb, B_fsb = AFa.alloc([16, 8], "fsb")
    fcs, B_fcs = AFa.alloc([136], "fcs")
    Fg, B_Fg = AFa.alloc([8, 136], "Fg")
    Dm, B_Dm = AFa.alloc([8, 8], "Dm")
    Am, B_Am = AFa.alloc([8, 8], "Am")

    pw_, pwb = PW.next()
    LFv = LF[:, 0:16, :].rearrange("p t h -> p (t h)")
    S.op("pe", lambda e: e.matmul(pw_[:, 0:128], triu_f[:, :], LFv, start=True, stop=True), [B_LF, B_ident], [pwb[0]])
    S.op("pe", lambda e: e.matmul(pw_[:, 512:640], ones_f[:, :], LFv, start=True, stop=True), [B_LF, B_ones], [pwb[1]])
    S.op("dve", lambda e: e.tensor_copy(fsa[:].rearrange("p t h -> p (t h)"), pw_[:, 512:640]), [pwb[1]], [B_fsa])
    src_, srcb, dst_, dstb = fsa, B_fsa, fsb, B_fsb
    for sh in (1, 2, 4, 8):
        S.op("dve", lambda e, s_=src_, d_=dst_, sh=sh: e.tensor_copy(d_[:, 0:sh, :], s_[:, 0:sh, :]), [srcb], [dstb])
        S.op("dve", lambda e, s_=src_, d_=dst_, sh=sh: e.tensor_tensor(d_[:, sh:16, :], s_[:, sh:16, :],
                                                                       s_[:, 0:16 - sh, :], ALU.add), [srcb], [dstb])
        src_, srcb, dst_, dstb = dst_, dstb, src_, srcb
    inc_, incb = src_, srcb
    S.op("dve", lambda e: e.tensor_copy(Cend[:], inc_[:]), [incb], [B_Ce])
    S.op("dve", lambda e: e.tensor_tensor(Fc[:].rearrange("p t h -> p (t h)"), pw_[:, 0:128],
                                          inc_[:].rearrange("p t h -> p (t h)"), ALU.add), [pwb[0], incb], [B_Fc])
    S.op("dve", lambda e: e.tensor_tensor(Fc[:].rearrange("p t h -> p (t h)"), Fc[:].rearrange("p t h -> p (t h)"),
                                          pw_[:, 512:640], ALU.subtract), [pwb[1], B_Fc], [B_Fc])
    S.op("dve", lambda e: e.tensor_copy(fcs[:, 0:128], Fc[:].rearrange("p t h -> p (t h)")), [B_Fc], [B_fcs])
    S.op("dve", lambda e: e.tensor_copy(fcs[:, 128:136], Cend[:, 15, :]), [B_Ce, B_fcs], [B_fcs])
    S.dma("sp", FCs.ap(), fcs[:], reads=[B_fcs], writes=[B_FCs])
    for ch in range(8):
        pr, half = ch // 2, ch % 2
        S.dma("sp", KBs[ch].ap(), KB_d.ap()[pr, :, half * 1024:(half + 1) * 1024], reads=[B_KB], writes=[B_KBs[ch]])
        S.dma("sp", VBs[ch].ap(), VB_d.ap()[ch * 256:(ch + 1) * 256, :], reads=[B_VB], writes=[B_VBs[ch]])

    def allgather(src_t, dst_t, sb_, db_):
        S.op("pool", lambda e: e.collective_compute("AllGather", ALU.bypass, replica_groups=[list(range(NCORES))],
                                                    ins=[src_t.ap().opt()], outs=[dst_t.ap().opt()]),
             [sb_], [db_], dma=True, inc=1)

    if 'ag' not in SKIP:
        allgather(FCs, FCr, B_FCs, B_FCr)
        for ch in range(8):
            allgather(KBs[ch], KBr[ch], B_KBs[ch], B_KBr[ch])
            allgather(VBs[ch], VBr[ch], B_VBs[ch], B_VBr[ch])
    else:
        S.op("pool", lambda e: e.memset(Fg[:], 0.0), [], [B_Fg])
    if 'ag' not in SKIP:
        S.dma("sp", Fg[:], FCr.ap().rearrange("(r p) c -> p r c", p=128), reads=[B_FCr], writes=[B_Fg])
    S.op("dve", lambda e: e.tensor_tensor(Am[:], Fg[:, :, 128:136], meta_sb[:, 1:9].unsqueeze(2).to_broadcast([128, 8, 8]),
                                          ALU.mult), [B_Fg, B_meta], [B_Am])
    S.op("dve", lambda e: e.tensor_copy(Dm[:, 7, :], Am[:, 7, :]), [B_Am], [B_Dm])
    for r in range(6, -1, -1):
        S.op("dve", lambda e, r=r: e.tensor_tensor(Dm[:, r, :], Dm[:, r + 1, :], Am[:, r, :], ALU.add),
             [B_Am, B_Dm], [B_Dm])
    S.op("dve", lambda e: e.tensor_tensor(Dm[:], Dm[:], meta_sb[:, 9:17].unsqueeze(2).to_broadcast([128, 8, 8]),
                                          ALU.add), [B_Dm, B_meta], [B_Dm])

    qb_r = Ring(S, "qb", 2, [4, 256], BF16, arena=AB)
    kb_r = Ring(S, "kbt", 3, [4, 256], BF16, arena=AB)
    vb_r = Ring(S, "vbt", 3, [2, 8, 128], BF16, arena=AB)
    pf_r = Ring(S, "pf", 4, [256], BF16, arena=AB)
    of_r = Ring(S, "of", 1, [4, 256], BF16, arena=AB)
    bm_r = Ring(S, "bm", 2, [2, 8, 2], F32, arena=AFa)
    rf_r = Ring(S, "rf", 2, [256], F32, arena=AFa)
    for t_, tb_ in zip(vb_r.t, vb_r.b):
        S.op("pool", lambda e, t_=t_: e.memset(t_[:], 1.0), [], [tb_])
    ACC = [(_pw[2 + hh // 4][:, ((hh // 2) % 2) * 512 + (hh % 2) * 256:((hh // 2) % 2) * 512 + (hh % 2) * 256 + 256],
            _pb[4 + hh // 2]) for hh in range(8)]
    SCR = _PRing([(_pw[k // 4][:, (k % 4) * 256:(k % 4) * 256 + 256], _pb[k // 2]) for k in range(8)])

    def fox_prompt_qtile(qi):
        q0 = qi * 256
        qt0 = qi * 2
        qb, qbb = qb_r.next()
        for pr in range(4):
            S.dma("sp", qb[:, pr, :], QB_d.ap()[pr, :, q0:q0 + 256], reads=[B_QB], writes=[qbb])
        first = [True] * 8
        units = [("g", r, kp) for r in range(8) for kp in range(8)] + [("o", None, kp) for kp in range(qi + 1)]
        for ui, (kind, r, kp) in enumerate(units):
            last_unit = (ui == len(units) - 1)
            kb, kbb = kb_r.next()
            vb, vbb = vb_r.next()
            bm, bmb = bm_r.next()
            for pr in range(4):
                if kind == "g":
                    ch = pr * 2 + kp // 4
                    S.dma("sp", kb[:, pr, :], KBr[ch].ap()[r * 128:(r + 1) * 128, (kp % 4) * 256:(kp % 4) * 256 + 256],
                          reads=[B_KBr[ch]], writes=[kbb])
                else:
                    S.dma("sp", kb[:, pr, :], KB_d.ap()[pr, :, kp * 256:(kp + 1) * 256], reads=[B_KB], writes=[kbb])
            if kind == "g":
                for j in range(2):
                    S.dma("sp", vb[:, j, :, 0:64],
                          VBr[kp].ap()[r * 256 + j * 128:r * 256 + (j + 1) * 128, :].rearrange("p (h d) -> p h d", d=64),
                          reads=[B_VBr[kp]], writes=[vbb])
                S.op("dve", lambda e, bm=bm, r=r, kp=kp: e.tensor_tensor(
                    bm[:], Cend[:, qt0:qt0 + 2, :].rearrange("p q h -> p h q").unsqueeze(1).to_broadcast([128, 2, 8, 2]),
                    Fg[:, r, kp * 16:kp * 16 + 16].rearrange("p (j h) -> p j h", j=2).unsqueeze(3).to_broadcast([128, 2, 8, 2]),
                    ALU.subtract), [B_Ce, B_Fg], [bmb])
                S.op("dve", lambda e, bm=bm, r=r: e.tensor_tensor(
                    bm[:], bm[:], Dm[:, r, :].unsqueeze(1).unsqueeze(3).to_broadcast([128, 2, 8, 2]), ALU.add),
                    [bmb, B_Dm], [bmb])
            else:
                for j in range(2):
                    S.dma("sp", vb[:, j, :, 0:64],
                          VB_d.ap()[kp * 256 + j * 128:kp * 256 + (j + 1) * 128, :].rearrange("p (h d) -> p h d", d=64),
                          reads=[B_VB], writes=[vbb])
                S.op("dve", lambda e, bm=bm, kp=kp: e.tensor_tensor(
                    bm[:], Cend[:, qt0:qt0 + 2, :].rearrange("p q h -> p h q").unsqueeze(1).to_broadcast([128, 2, 8, 2]),
                    Fc[:, kp * 2:kp * 2 + 2, :].unsqueeze(3).to_broadcast([128, 2, 8, 2]), ALU.subtract),
                    [B_Ce, B_Fc], [bmb])
            for j in range(2):
                kt = kp * 2 + j
                if kind == "o" and kt > qt0 + 1:
                    continue
                cs = 0
                if kind == "o" and kt == qt0 + 1:
                    cs = 128
                for h in range(8):
                    pr, r0 = h // 2, (h % 2) * 64
                    sc, scb = SCR.next()
                    S.op("pe", lambda e, sc=sc, kb=kb, pr=pr, r0=r0, j=j, cs=cs: e.matmul(
                        sc[:, cs:256], kb[r0:r0 + 64, pr, j * 128:(j + 1) * 128], qb[r0:r0 + 64, pr, cs:256],
                        start=True, stop=True), [kbb, qbb], [scb])
                    pf, pfb = pf_r.next()
                    for sub in range(cs // 128, 2):
                        S.op("act", lambda e, pf=pf, sc=sc, bm=bm, j=j, h=h, sub=sub: e.activation(
                            pf[:, sub * 128:(sub + 1) * 128], sc[:, sub * 128:(sub + 1) * 128], AF.Exp,
                            bias=bm[:, j, h, sub:sub + 1], scale=0.125), [scb, bmb, pfb] if sub else [scb, bmb], [pfb])
                        if kind == "o" and kt == qt0 + sub:
                            S.op("dve", lambda e, pf=pf, sub=sub: e.tensor_tensor(
                                pf[:, sub * 128:(sub + 1) * 128], pf[:, sub * 128:(sub + 1) * 128], triu_b[:, :],
                                ALU.mult), [pfb, B_ident], [pfb])
                    acc, accb = ACC[h]
                    is_last = last_unit and (j == 1 or (kind == "o" and kt + 1 > qt0 + 1)) and (h % 2 == 1)
                    S.op("pe", lambda e, acc=acc, vb=vb, j=j, h=h, pf=pf, cs=cs, st_=(first[h] and h % 2 == 0),
                         sp_=is_last: e.matmul(
                        acc[:, cs:256], vb[:, j, h, :], pf[:, cs:256], start=st_, stop=sp_), [vbb, pfb], [accb])
                    first[h] = False
        of, ofb = of_r.next()
        for h in range(8):
            pr, r0 = h // 2, (h % 2) * 64
            acc, accb = ACC[h]
            rf, rfb = rf_r.next()
            S.op("dve", lambda e, rf=rf, acc=acc: e.reciprocal(rf[0:64, :], acc[64:128, :]), [accb], [rfb])
            S.op("dve", lambda e, rf=rf, acc=acc, pr=pr, r0=r0: e.tensor_tensor(
                of[r0:r0 + 64, pr, :], acc[0:64, :], rf[0:64, :], ALU.mult), [accb, rfb], [ofb])
        for pr in range(4):
            S.dma("sp", O_d.ap()[4 + pr, :, q0:q0 + 256], of[:, pr, :], reads=[ofb], writes=[B_O])

    NQ = int(FOXQ) if FOXQ is not None else 8
    for qi in range(NQ):
        fox_prompt_qtile(qi)

    def fox_sample_all():
        AB.reset()
        AFa.reset()
        oTs2, B_oTs2 = AB.alloc([4, TS], "oTs2")
        clf_r = Ring(S, "clf", 1, [8, 8], F32, arena=AFa)
        fa_r = Ring(S, "fa", 2, [8, 8], F32, arena=AFa)
        fcc_r = Ring(S, "fcc", 1, [8, 8], F32, arena=AFa)
        fn_r = Ring(S, "fn", 1, [24], F32, arena=AFa)
        bmc_r = Ring(S, "bmc", 1, [8, 8], F32, arena=AFa)
        bmn_r = Ring(S, "bmn", 1, [8], F32, arena=AFa)
        ckf_r2 = Ring(S, "ckf2", 1, [8, 512], F32, arena=AFa)
        kcT_r2 = Ring(S, "kcT2", 1, [4, 1024], BF16, arena=AB)
        cv_r2 = Ring(S, "cv2", 1, [8, 512], BF16, arena=AB)
        qs_r2 = Ring(S, "qs2", 1, [4, 16], BF16, arena=AB)
        kn_r2 = Ring(S, "kn2", 1, [4, 16], BF16, arena=AB)
        vn_r2 = Ring(S, "vn2", 1, [512], BF16, arena=AB)
        ps_r2 = Ring(S, "pts2", 2, [144], BF16, arena=AB)
        rc_r2 = Ring(S, "rc2", 2, [16], F32, arena=AFa)
        sel, B_sel = AFa.alloc([4, 16], "sel")
        sela, B_sela = AFa.alloc([4, 128], "sela")
        for bb in range(4):
            S.op("pool", lambda e, bb=bb: e.memset(sel[0:64, bb, :], 1.0), [], [B_sel])
            S.op("pool", lambda e, bb=bb: e.affine_select(out=sel[0:64, bb, :], in_=sel[0:64, bb, :], compare_op=ALU.is_ge,
                                                          fill=0.0, base=16 * bb, pattern=[[1, 16]], channel_multiplier=-1),
                 [B_sel], [B_sel])
            S.op("pool", lambda e, bb=bb: e.affine_select(out=sel[0:64, bb, :], in_=sel[0:64, bb, :], compare_op=ALU.is_ge,
                                                          fill=0.0, base=-16 * bb, pattern=[[0, 16]], channel_multiplier=1),
                 [B_sel], [B_sel])
            S.op("pool", lambda e, bb=bb: e.memset(sela[0:64, bb, :], 1.0), [], [B_sela])
            S.op("pool", lambda e, bb=bb: e.affine_select(out=sela[0:64, bb, :], in_=sela[0:64, bb, :], compare_op=ALU.is_ge,
                                                          fill=0.0, base=16 * bb + 15, pattern=[[0, 128]],
                                                          channel_multiplier=-1), [B_sela], [B_sela])
            S.op("pool", lambda e, bb=bb: e.affine_select(out=sela[0:64, bb, :], in_=sela[0:64, bb, :], compare_op=ALU.is_ge,
                                                          fill=0.0, base=-16 * bb, pattern=[[0, 128]], channel_multiplier=1),
                 [B_sela], [B_sela])
        for b in range(4):
            clf, clfb = clf_r.next()
            S.dma("sp", clf[:], c_bl.ap()[b].rearrange("(t p) h -> p t h", p=128), writes=[clfb])
            clfv = clf[:].rearrange("p t h -> p (t h)")
            pw_, pwb = PW.next()
            S.op("pe", lambda e, pw_=pw_, clfv=clfv: e.matmul(pw_[:, 0:64], triu_f[:, :], clfv, start=True, stop=True),
                 [clfb, B_ident], [pwb[0]])
            S.op("pe", lambda e, pw_=pw_, clfv=clfv: e.matmul(pw_[:, 512:576], ones_f[:, :], clfv, start=True, stop=True),
                 [clfb, B_ones], [pwb[1]])
            fa, fab = fa_r.next()
            fb, fbb = fa_r.next()
            S.op("dve", lambda e, fa=fa, pw_=pw_: e.tensor_copy(fa[:].rearrange("p t h -> p (t h)"), pw_[:, 512:576]),
                 [pwb[1]], [fab])
            s_, sb_, d_, db_ = fa, fab, fb, fbb
            for sh in (1, 2, 4):
                S.op("dve", lambda e, s_=s_, d_=d_, sh=sh: e.tensor_copy(d_[:, 0:sh, :], s_[:, 0:sh, :]), [sb_], [db_])
                S.op("dve", lambda e, s_=s_, d_=d_, sh=sh: e.tensor_tensor(d_[:, sh:8, :], s_[:, sh:8, :], s_[:, 0:8 - sh, :],
                                                                         ALU.add), [sb_], [db_])
                s_, sb_, d_, db_ = d_, db_, s_, sb_
            inc_, incb = s_, sb_
            fcc, fccb = fcc_r.next()
            S.op("dve", lambda e, fcc=fcc, pw_=pw_, inc_=inc_: e.tensor_tensor(
                fcc[:].rearrange("p t h -> p (t h)"), pw_[:, 0:64], inc_[:].rearrange("p t h -> p (t h)"), ALU.add),
                [pwb[0], incb], [fccb])
            S.op("dve", lambda e, fcc=fcc, pw_=pw_: e.tensor_tensor(
                fcc[:].rearrange("p t h -> p (t h)"), fcc[:].rearrange("p t h -> p (t h)"), pw_[:, 512:576], ALU.subtract),
                [pwb[1], fccb], [fccb])
            fn, fnb = fn_r.next()
            S.op("dve", lambda e, fn=fn, inc_=inc_: e.tensor_copy(fn[:, 16:24], inc_[:, 7, :]), [incb], [fnb])
            p1, p1b = PS.next()
            S.op("pe", lambda e, p1=p1, b=b: e.matmul(p1[0:16, 0:8], sel[0:64, b, :], LF[0:64, 16, :], start=True, stop=True),
                 [B_sel, B_LF], [p1b])
            S.op("pe", lambda e, p1=p1, b=b: e.matmul(p1[:, 8:16], sela[0:64, b, :], LF[0:64, 16, :], start=True, stop=True),
                 [B_sela, B_LF], [p1b])
            S.op("dve", lambda e, fn=fn, p1=p1: e.tensor_tensor(fn[0:16, 0:8], p1[0:16, 0:8], fn[0:16, 16:24], ALU.add),
                 [p1b, fnb], [fnb])
            S.op("dve", lambda e, fn=fn, p1=p1: e.tensor_tensor(fn[:, 8:16], p1[:, 8:16], fn[:, 16:24], ALU.add),
                 [p1b, fnb], [fnb])
            bmc, bmcb = bmc_r.next()
            bmn, bmnb = bmn_r.next()
            S.op("dve", lambda e, bmc=bmc, fn=fn, fcc=fcc: e.tensor_tensor(
                bmc[:], fn[:, 8:16].unsqueeze(1).to_broadcast([128, 8, 8]), fcc[:], ALU.subtract), [fnb, fccb], [bmcb])
            S.op("dve", lambda e, bmn=bmn, fn=fn: e.tensor_tensor(bmn[0:16, :], fn[0:16, 8:16], fn[0:16, 0:8], ALU.subtract),
                 [fnb], [bmnb])
            ckf, ckfb = ckf_r2.next()
            S.dma("sp", ckf[:], c_bk.ap()[b].rearrange("(t p) c -> p t c", p=128), writes=[ckfb])
            cv, cvb = cv_r2.next()
            S.dma("pool", cv[:], c_bv.ap()[b].rearrange("(t p) c -> p t c", p=128), writes=[cvb])
            qs, qsb = qs_r2.next()
            kn, knb = kn_r2.next()
            vn, vnb = vn_r2.next()
            for pr in range(4):
                S.dma("sp", qs[:, pr, :], QB_d.ap()[pr, :, TP + b * 16:TP + (b + 1) * 16], reads=[B_QB], writes=[qsb])
                S.dma("sp", kn[:, pr, :], KB_d.ap()[pr, :, TP + b * 16:TP + (b + 1) * 16], reads=[B_KB], writes=[knb])
            S.dma("sp", vn[0:16, :], VB_d.ap()[TP + b * 16:TP + (b + 1) * 16, :], reads=[B_VB], writes=[vnb])
            kcT, kcTb = kcT_r2.next()
            for pr in range(4):
                for half in range(2):
                    p_, pb_ = PS.next()
                    for t in range(4):
                        S.op("pe", lambda e, p_=p_, t=t, pr=pr, half=half, ckf=ckf: e.transpose(
                            p_[:, t * 128:(t + 1) * 128], ckf[:, half * 4 + t, pr * 128:(pr + 1) * 128], ident[:, :]),
                            [ckfb, B_ident], [pb_])
                    evac(kcT[:, pr, half * 512:(half + 1) * 512], p_[:, 0:512], [pb_], [kcTb])
            for h in range(8):
                pr, r0 = h // 2, (h % 2) * 64
                p_, pb_ = PS.next()
                for t in range(8):
                    S.op("pe", lambda e, p_=p_, t=t, pr=pr, r0=r0: e.matmul(
                        p_[:, t * 16:(t + 1) * 16], kcT[r0:r0 + 64, pr, t * 128:(t + 1) * 128], qs[r0:r0 + 64, pr, :],
                        start=True, stop=True), [kcTb, qsb], [pb_])
                S.op("pe", lambda e, p_=p_, pr=pr, r0=r0: e.matmul(
                    p_[0:16, 128:144], kn[r0:r0 + 64, pr, :], qs[r0:r0 + 64, pr, :], start=True, stop=True),
                    [knb, qsb], [pb_])
                pt, ptb = ps_r2.next()
                for t in range(8):
                    S.op("act", lambda e, pt=pt, p_=p_, t=t, h=h, bmc=bmc: e.activation(
                        pt[:, t * 16:(t + 1) * 16], p_[:, t * 16:(t + 1) * 16], AF.Exp, bias=bmc[:, t, h:h + 1],
                        scale=0.125), [pb_, bmcb] + ([ptb] if t else []), [ptb])
                S.op("act", lambda e, pt=pt, p_=p_, h=h, bmn=bmn: e.activation(
                    pt[0:16, 128:144], p_[0:16, 128:144], AF.Exp, bias=bmn[0:16, h:h + 1], scale=0.125),
                    [pb_, bmnb, ptb], [ptb])
                S.op("dve", lambda e, pt=pt: e.tensor_tensor(pt[0:16, 128:144], pt[0:16, 128:144], triu_b[0:16, 0:16],
                                                             ALU.mult), [ptb, B_ident], [ptb])
                po, pob = PA.next()
                pd, pdb = PA.next()
                for t in range(8):
                    S.op("pe", lambda e, po=po, t=t, h=h, pt=pt: e.matmul(
                        po[0:64, 0:16], cv[:, t, h * 64:(h + 1) * 64], pt[:, t * 16:(t + 1) * 16],
                        start=(t == 0), stop=False), [cvb, ptb], [pob])
                    S.op("pe", lambda e, pd=pd, t=t, pt=pt: e.matmul(
                        pd[0:64, 0:16], ones_b[:, 0:64], pt[:, t * 16:(t + 1) * 16], start=(t == 0), stop=False),
                        [ptb, B_ones], [pdb])
                S.op("pe", lambda e, po=po, h=h, pt=pt: e.matmul(
                    po[0:64, 0:16], vn[0:16, h * 64:(h + 1) * 64], pt[0:16, 128:144], start=False, stop=True),
                    [vnb, ptb], [pob])
                S.op("pe", lambda e, pd=pd, pt=pt: e.matmul(
                    pd[0:64, 0:16], ones_b[0:16, 0:64], pt[0:16, 128:144], start=False, stop=True), [ptb, B_ones], [pdb])
                rc, rcb = rc_r2.next()
                S.op("dve", lambda e, rc=rc, pd=pd: e.reciprocal(rc[0:64, 0:16], pd[0:64, 0:16]), [pdb], [rcb])
                S.op("dve", lambda e, rc=rc, po=po, pr=pr, r0=r0, b=b: e.tensor_tensor(
                    oTs2[r0:r0 + 64, pr, b * 16:(b + 1) * 16], po[0:64, 0:16], rc[0:64, 0:16], ALU.mult),
                    [pob, rcb], [B_oTs2])
        for pr in range(4):
            S.dma("sp", O_d.ap()[4 + pr, :, TP:TP + TS], oTs2[:, pr, :], reads=[B_oTs2], writes=[B_O])

    if 'foxs' not in SKIP:
        fox_sample_all()
        AB.reset()
        AFa.reset()
        dbt_r = Ring(S, "dbt", 2, [NOWN], BF16, arena=AB)
        if 'band' in SKIP:
            for kt in range(4, 8):
                t_, tb_ = dbt_r.next()
                S.dma("sp", t_[:, 0:TS], O_d.ap()[kt, :, TP:TP + TS], reads=[B_O], writes=[tb_])
                S.dma("sp", o_dbg.ap()[kt, :, TP:TP + TS], t_[:, 0:TS], reads=[tb_])

    dbt_r = Ring(S, "dbt", 2, [NOWN], BF16, arena=AB)
    for kt in range(KT):
        if 'band' in SKIP and kt < 4:
            continue
        ncol = NOWN if (NQ == 8 and 'band' not in SKIP) else 256 * NQ
        if ncol == 0:
            continue
        t_, tb_ = dbt_r.next()
        S.dma("sp", t_[:, 0:ncol], O_d.ap()[kt, :, 0:ncol], reads=[B_O], writes=[tb_])
        S.dma("sp", o_dbg.ap()[kt, :, 0:ncol], t_[:, 0:ncol], reads=[tb_])

    def out_proj(w_dram, src_d, B_src):
        AB.reset()
        AFa.reset()
        wo, wob = AB.alloc([KT, D], "wo")
        S.dma("pool", wo[:], w_dram.rearrange("(kt p) m -> p kt m", p=128), writes=[wob])
        oin_r = Ring(S, "oin", 2, [KT, 512], BF16, arena=AB)
        for (c0, T, xb) in TILES:
            oin, oinb = oin_r.next()
            for kt in range(KT):
                S.dma("sp", oin[:, kt, 0:T], src_d.ap()[kt, :, c0:c0 + T], reads=[B_src], writes=[oinb])
            for o in range(KT):
                py, pyb = PS.next()
                for kt in range(KT):
                    S.op("pe", lambda e, py=py, kt=kt, o=o, oin=oin, T=T: e.matmul(
                        py[:, 0:T], wo[:, kt, o * 128:(o + 1) * 128], oin[:, kt, 0:T], start=(kt == 0),
                        stop=(kt == KT - 1)), [wob, oinb], [pyb])
                S.op("dve", lambda e, py=py, o=o, c0=c0, T=T: e.tensor_tensor(
                    xT[:, o, c0:c0 + T], py[:, 0:T], xT[:, o, c0:c0 + T], ALU.add), [pyb, xb], [xb])

    if stage_end >= 2 and 'mixo' not in SKIP:
        out_proj(ab_w_o.ap()[0], O_d, B_O)

    X_d = dscr("X_d", [KT, 128, NOWN])
    B_X = Buf("X_d")

    def cross_attn(l):
        AB.reset()
        AFa.reset()
        hT_ = Ring(S, "hTx", 1, [KT, 512], BF16, arena=AB)
        wq, wqb = AB.alloc([KT, D], "wq")
        mkT, mkTb = AB.alloc([KT, 256], "mkT")
        mvb, mvbb = AB.alloc([2, D], "mvb")
        qx_r = Ring(S, "qx", 1, [KT, 512], BF16, arena=AB)
        px_r = Ring(S, "px", 2, [2, 512], BF16, arena=AB)
        ox_r = Ring(S, "ox", 1, [KT, 512], BF16, arena=AB)
        wt_r = Ring(S, "wtx", 2, [KT, 128], BF16, arena=AB)
        memT, B_mem = AFa.alloc([KT, 256], "memTx")
        sq_ = Ring(S, "sqx", 1, [KT, 512], F32, arena=AFa)
        rs_ = Ring(S, "rsx", 2, [512], F32, arena=AFa)
        xi_ = Ring(S, "xix", 1, [D], F32, arena=AFa)
        R["sq"], R["rstd"], R["hT"], R["xin"] = sq_, rs_, hT_, xi_
        S.dma("pool", wq[:], xa_wq.ap()[l].rearrange("(kt p) m -> p kt m", p=128), writes=[wqb])
        for i in range(2):
            load_transposed(mem_in.ap()[i * 128:(i + 1) * 128, :], 128, memT, i * 128, [B_mem])
        mh, mhb = rmsnorm(memT, 0, 256, [B_mem], 8 + l)
        for ct in range(KT):
            w_, wb_ = wt_r.next()
            S.dma("pool", w_[:], xa_wk.ap()[l][:, ct * 128:(ct + 1) * 128].rearrange("(kt p) m -> p kt m", p=128),
                  writes=[wb_])
            p_, pb_ = PS.next()
            for kt in range(KT):
                S.op("pe", lambda e, p_=p_, kt=kt, w_=w_: e.matmul(p_[:, 0:256], w_[:, kt, :], mh[:, kt, 0:256],
                                                                  start=(kt == 0), stop=(kt == KT - 1)), [wb_, mhb], [pb_])
            evac(mkT[:, ct, :], p_[:, 0:256], [pb_], [mkTb])
        for ct in range(KT):
            w_, wb_ = wt_r.next()
            S.dma("pool", w_[:], xa_wv.ap()[l][:, ct * 128:(ct + 1) * 128].rearrange("(kt p) m -> p kt m", p=128),
                  writes=[wb_])
            for j in range(2):
                p_, pb_ = PS.next()
                for kt in range(KT):
                    S.op("pe", lambda e, p_=p_, kt=kt, w_=w_, j=j: e.matmul(
                        p_[:, 0:128], mh[:, kt, j * 128:(j + 1) * 128], w_[:, kt, :], start=(kt == 0),
                        stop=(kt == KT - 1)), [wb_, mhb], [pb_])
                evac(mvb[:, j, ct * 128:(ct + 1) * 128], p_[:, 0:128], [pb_], [mvbb])

        def attend(qx, qxb, c_lo, T, kT, kTb, vv, vvb, ox, oxb):
            for hx in range(4):
                px, pxb = px_r.next()
                for kt in range(2):
                    ps_, psb = PS.next()
                    for dt in range(2):
                        S.op("pe", lambda e, ps_=ps_, kt=kt, dt=dt, hx=hx: e.matmul(
                            ps_[:, 0:T], kT[:, hx * 2 + dt, kt * 128:(kt + 1) * 128], qx[:, hx * 2 + dt, c_lo:c_lo + T],
                            start=(dt == 0), stop=(dt == 1)), [kTb, qxb], [psb])
                    S.op("act", lambda e, px=px, ps_=ps_, kt=kt: e.activation(px[:, kt, 0:T], ps_[:, 0:T], AF.Exp,
                                                                             scale=1.0 / 16.0),
                         [psb] + ([pxb] if kt else []), [pxb])
                pd, pdb = PA.next()
                for kt in range(2):
                    S.op("pe", lambda e, pd=pd, kt=kt, px=px: e.matmul(pd[:, 0:T], ones_b[:, :], px[:, kt, 0:T],
                                                                      start=(kt == 0), stop=(kt == 1)), [pxb, B_ones], [pdb])
                rc, rcb = rs_.next()
                S.op("dve", lambda e, rc=rc, pd=pd: e.reciprocal(rc[:, 0:T], pd[:, 0:T]), [pdb], [rcb])
                for dt in range(2):
                    po, pob = PS.next()
                    for kt in range(2):
                        S.op("pe", lambda e, po=po, kt=kt, dt=dt, hx=hx, px=px: e.matmul(
                            po[:, 0:T], vv[:, kt, hx * 256 + dt * 128:hx * 256 + (dt + 1) * 128], px[:, kt, 0:T],
                            start=(kt == 0), stop=(kt == 1)), [vvb, pxb], [pob])
                    S.op("dve", lambda e, po=po, rc=rc, hx=hx, dt=dt: e.tensor_tensor(
                        ox[:, hx * 2 + dt, c_lo:c_lo + T], po[:, 0:T], rc[:, 0:T], ALU.mult), [pob, rcb], [oxb])

        ckf_ = Ring(S, "ckfx", 1, [2, D], F32, arena=AFa)
        ckT_ = Ring(S, "ckTx", 1, [KT, 256], BF16, arena=AB)
        cvx_ = Ring(S, "cvx", 1, [2, D], BF16, arena=AB)
        for ti, (c0, T, xb) in enumerate(TILES):
            h, hb = rmsnorm(xT, c0, T, [xb], l * 4 + 2)
            qx, qxb = qx_r.next()
            for ct in range(KT):
                p_, pb_ = PS.next()
                for kt in range(KT):
                    S.op("pe", lambda e, p_=p_, kt=kt, ct=ct, h=h, T=T: e.matmul(
                        p_[:, 0:T], wq[:, kt, ct * 128:(ct + 1) * 128], h[:, kt, 0:T], start=(kt == 0),
                        stop=(kt == KT - 1)), [wqb, hb], [pb_])
                evac(qx[:, ct, 0:T], p_[:, 0:T], [pb_], [qxb])
            ox, oxb = ox_r.next()
            if ti < 4:
                attend(qx, qxb, 0, T, mkT, mkTb, mvb, mvbb, ox, oxb)
            else:
                for bb in range(4):
                    ckf, ckfb = ckf_.next()
                    S.dma("sp", ckf[:], c_mk.ap()[l, bb].rearrange("(t p) c -> p t c", p=128), writes=[ckfb])
                    cvx, cvxb = cvx_.next()
                    S.dma("pool", cvx[:], c_mv.ap()[l, bb].rearrange("(t p) c -> p t c", p=128), writes=[cvxb])
                    ckT, ckTb = ckT_.next()
                    for half in range(2):
                        for t in range(2):
                            p_, pb_ = PS.next()
                            for j4 in range(4):
                                dt8 = half * 4 + j4
                                S.op("pe", lambda e, p_=p_, j4=j4, dt8=dt8, t=t, ckf=ckf: e.transpose(
                                    p_[:, j4 * 128:(j4 + 1) * 128], ckf[:, t, dt8 * 128:(dt8 + 1) * 128], ident[:, :]),
                                    [ckfb, B_ident], [pb_])
                            evac(ckT[:, half * 4:half * 4 + 4, t * 128:(t + 1) * 128],
                                 p_[:, 0:512].rearrange("p (j k) -> p j k", j=4), [pb_], [ckTb])
                    attend(qx, qxb, bb * 16, 16, ckT, ckTb, cvx, cvxb, ox, oxb)
            for kt in range(KT):
                S.dma("sp", X_d.ap()[kt, :, c0:c0 + T], ox[:, kt, 0:T], reads=[oxb], writes=[B_X])

    if stage_end >= 2 and 'l0x' not in SKIP:
        cross_attn(0)
        out_proj(xa_wo.ap()[0], X_d, B_X)

    if stage_end >= 3:
        alloc_ffn()
        for (c0, T, xb) in TILES:
            ffn(0, 1, 3, xT, c0, T, [xb])
        for (c0, T, xb) in TILES:
            ffn(1, 0, 4, xT, c0, T, [xb])
        gwin = gdn_w_in.ap()[0]
        for (c0, T, xb, r0, nrows) in ((1536, 512, B_x[3], 384, 128), (TP, TS, B_x[4], 0, TS)):
            h, hb = rmsnorm(xT, c0, T, [xb], 5)
            for g in range(6):
                wv, wb = load_w512(gwin, g * 512, 512)
                p, pb = tokmajor(h, hb, r0, nrows, wv, wb, 512)
                st, sb_ = R["stage"].next()
                evac(st[0:nrows, 0:512], p[0:nrows, 0:512], [pb], [sb_])
                if c0 == 1536:
                    S.dma("sp", o_gcp.ap()[:, g * 512:(g + 1) * 512], st[125:128, 0:512], reads=[sb_])
                    S.dma("sp", GCs.ap()[:, g * 512:(g + 1) * 512], st[125:128, 0:512], reads=[sb_], writes=[B_GCs])
                else:
                    for bb in range(4):
                        S.dma("sp", o_gcs.ap()[bb, :, g * 512:(g + 1) * 512], st[bb * 16 + 13:bb * 16 + 16, 0:512],
                              reads=[sb_])

    def gdn_phase():
        AB.reset()
        AFa.reset()
        R["hT"] = Ring(S, "hTg", 1, [KT, 512], BF16, arena=AB)
        gsm, B_gsm = AFa.alloc([32], "gsm")
        Sst, B_Sst = AFa.alloc([8, 128], "Sst")
        R["sq"] = Ring(S, "sqg", 1, [KT, 256], F32, arena=AFa)
        R["rstd"] = Ring(S, "rsg", 2, [512], F32, arena=AFa)
        R["wgu"] = Ring(S, "w128g", 2, [2, KT, 128], BF16, arena=AB)
        R["wdn"] = Ring(S, "w512g", 1, [FT, 128], BF16, arena=AB)
        qhT, B_qh = AB.alloc([8, 512], "qhT")
        khT, B_kh = AB.alloc([8, 512], "khT")
        vT, B_vT = AB.alloc([8, 512], "vT")
        Xb, B_Xb = AB.alloc([8, 256], "Xb")
        Xf, B_Xf = AFa.alloc([8, 256], "Xf")
        carry, B_cy = AFa.alloc([24, 3], "carry")
        cw, B_cw = AFa.alloc([24, 4], "cw")
        Gt, B_Gt = AFa.alloc([17, 16], "Gt")
        gcc, B_gcc = AFa.alloc([24], "gcc")
        xp_r = Ring(S, "xp", 2, [515], F32, arena=AFa)
        ac_r = Ring(S, "acg", 2, [512], F32, arena=AFa)
        names_f = "gT grow diff tA DU W1 tB DL bt Nf NTf Rf P2 PT2 ktf uf".split()
        uf_off = AFa.off
        UF = {n: AFa.alloc([128], "u_" + n) for n in names_f}
        UF["idb"] = UF["gT"]
        ALLUF = [UF[n][1] for n in names_f]
        names_b = "AqkT TTb kbg kg vb wT qgT".split()
        UB = {n: AB.alloc([128], "u_" + n) for n in names_b}
        vnx, B_vnx = AB.alloc([256], "vnx")
        ost_r = ac_r
        osb_r = Ring(S, "ostb", 2, [128], BF16, arena=AB)
        for j in range(4):
            S.dma("sp", cw[:, :, j], gdn_conv_w.ap()[0, j].rearrange("(ct p) -> p ct", p=128), writes=[B_cw],
                  allow_slow_non_contiguous=True)
        S.dma("sp", gsm[:, 0:8], gdn_dt_bias.ap()[0:1, :].partition_broadcast(128), writes=[B_gsm])
        S.dma("sp", gsm[:, 8:16], gdn_a_log.ap()[0:1, :].partition_broadcast(128), writes=[B_gsm])
        S.dma("sp", gsm[:, 16:17], gdn_norm_g.ap().rearrange("o p -> p o"), writes=[B_gsm], allow_slow_non_contiguous=True)
        S.op("act", lambda e: e.activation(gsm[:, 8:16], gsm[:, 8:16], AF.Exp), [B_gsm], [B_gsm])
        S.op("dve", lambda e: e.tensor_scalar(gsm[:, 8:16], gsm[:, 8:16], -1.0, None, ALU.mult), [B_gsm], [B_gsm])
        gwin = gdn_w_in.ap()[0]

        def mm(out_ap, lhsT, rhs, reads, writes, start=True, stop=True):
            S.op("pe", lambda e: e.matmul(out_ap, lhsT, rhs, start=start, stop=stop), reads, writes)

        def unit(C, h, c0, g_col, b_col, ext, ol_dst, op_dst, gbuf=None):
            gbuf = gbuf if gbuf is not None else B_Gt
            f = lambda n: UF[n][0][0:C, 0:C]
            fb = lambda n: UF[n][1]
            qc, kc, vc = qhT[:, h, c0:c0 + C], khT[:, h, c0:c0 + C], vT[:, h, c0:c0 + C]
            NX = 256 if ext else 128
            S.op("dve", lambda e: e.tensor_scalar(f("gT"), triu_f[0:C, 0:C], g_col, None, ALU.mult), [B_ident, gbuf], [fb("gT")])
            p1, p1b = PS.next()
            mm(p1[:, 0:C], ones_f[0:C, :], f("gT"), [fb("gT"), B_ones], [p1b])
            evac(UF["grow"][0][:, 0:C], p1[:, 0:C], [p1b], [fb("grow")])
            S.op("dve", lambda e: e.tensor_scalar(f("diff"), f("grow"), gcc[0:C, h:h + 1], None, ALU.subtract),
                 [fb("grow"), B_gcc], [fb("diff")])
            S.op("dve", lambda e: e.tensor_scalar(f("tA"), f("diff"), 0.0, None, ALU.min), [fb("diff")], [fb("tA")])
            S.op("act", lambda e: e.activation(f("tA"), f("tA"), AF.Exp), [fb("tA")], [fb("tA")])
            S.op("dve", lambda e: e.tensor_tensor(f("DU"), f("tA"), triu_f[0:C, 0:C], ALU.mult), [fb("tA"), B_ident], [fb("DU")])
            S.op("dve", lambda e: e.tensor_scalar(f("tB"), f("diff"), -1.0, 0.0, ALU.mult, ALU.min), [fb("diff")], [fb("tB")])
            S.op("act", lambda e: e.activation(f("tB"), f("tB"), AF.Exp), [fb("tB")], [fb("tB")])
            S.op("dve", lambda e: e.tensor_tensor(f("DL"), f("tB"), tril_s[0:C, 0:C], ALU.mult), [fb("tB"), B_ident], [fb("DL")])
            S.op("dve", lambda e: e.tensor_scalar(f("idb"), ident[0:C, 0:C], b_col, None, ALU.mult), [B_ident, gbuf], [fb("idb")])
            p2, p2b = PS.next()
            mm(p2[0:C, 0:C], ones_f[0:C, 0:C], f("idb"), [fb("idb"), B_ones], [p2b])
            evac(f("bt"), p2[0:C, 0:C], [p2b], [fb("bt")])
            S.op("dve", lambda e: e.tensor_tensor(f("W1"), f("tA"), triu_s[0:C, 0:C], ALU.mult), [fb("tA"), B_ident], [fb("W1")])
            S.op("dve", lambda e: e.tensor_tensor(f("W1"), f("W1"), f("bt"), ALU.mult), [fb("W1"), fb("bt")], [fb("W1")])
            p3, p3b = PS.next()
            mm(p3[0:C, 0:C], kc, kc, [B_kh], [p3b])
            S.op("dve", lambda e: e.tensor_tensor(f("Nf"), p3[0:C, 0:C], f("W1"), ALU.mult), [p3b, fb("W1")], [fb("Nf")])
            p4, p4b = PS.next()
            mm(p4[0:C, 0:C], kc, kc, [B_kh], [p4b])
            S.op("dve", lambda e: e.scalar_tensor_tensor(out=f("NTf"), in0=p4[0:C, 0:C], scalar=b_col, in1=f("DL"),
                                                         op0=ALU.mult, op1=ALU.mult), [p4b, fb("DL"), gbuf], [fb("NTf")])
            p5, p5b = PS.next()
            mm(p5[0:C, 0:C], kc, qc, [B_kh, B_qh], [p5b])
            aq, aqb = UB["AqkT"]
            S.op("dve", lambda e: e.tensor_tensor(aq[0:C, 0:C], p5[0:C, 0:C], f("DU"), ALU.mult), [p5b, fb("DU")], [aqb])
            S.op("dve", lambda e: e.tensor_tensor(f("Rf"), ident[0:C, 0:C], f("Nf"), ALU.subtract), [B_ident, fb("Nf")], [fb("Rf")])
            P, Pb, PT, PTb = f("Nf"), fb("Nf"), f("NTf"), fb("NTf")
            Q, Qb, QT, QTb = f("P2"), fb("P2"), f("PT2"), fb("PT2")
            nlev = {128: 6, 16: 3}[C]
            for lev in range(nlev):
                pa, pab = PS.next()
                mm(pa[0:C, 0:C], P, PT, [Pb, PTb], [pab])
                evac(QT, pa[0:C, 0:C], [pab], [QTb])
                if lev < nlev - 1:
                    pb_, pbb = PS.next()
                    mm(pb_[0:C, 0:C], PT, P, [Pb, PTb], [pbb])
                    evac(Q, pb_[0:C, 0:C], [pbb], [Qb])
                pc, pcb = PS.next()
                mm(pc[0:C, 0:C], QT, f("Rf"), [QTb, fb("Rf")], [pcb])
                S.op("dve", lambda e, pc=pc: e.tensor_tensor(f("Rf"), f("Rf"), pc[0:C, 0:C], ALU.add), [pcb, fb("Rf")], [fb("Rf")])
                P, Pb, PT, PTb, Q, Qb, QT, QTb = Q, Qb, QT, QTb, P, Pb, PT, PTb
            tt, ttb = UB["TTb"]
            S.op("act", lambda e: e.copy(tt[0:C, 0:C], f("Rf")), [fb("Rf")], [ttb])
            p6, p6b = PS.next()
            mm(p6[0:C, 0:128], kc, ident_b[:, :], [B_kh, B_ident], [p6b])
            ktf, ktfb = UF["ktf"]
            evac(ktf[0:C, :], p6[0:C, 0:128], [p6b], [ktfb])
            S.op("dve", lambda e: e.tensor_scalar(gcc[0:C, 16:17], f("grow")[:, C - 1:C], gcc[0:C, h:h + 1], None, ALU.subtract),
                 [fb("grow"), B_gcc], [B_gcc])
            S.op("act", lambda e: e.activation(gcc[0:C, 16:17], gcc[0:C, 16:17], AF.Exp), [B_gcc], [B_gcc])
            S.op("act", lambda e: e.activation(gcc[:, 17:18], UF["grow"][0][:, C - 1:C], AF.Exp), [fb("grow"), B_gcc], [B_gcc])
            kbg, kbgb = UB["kbg"]
            kg, kgb = UB["kg"]
            S.op("dve", lambda e: e.tensor_scalar(kbg[0:C, :], ktf[0:C, :], gcc[0:C, 8 + h:9 + h], None, ALU.mult),
                 [ktfb, B_gcc], [kbgb])
            S.op("dve", lambda e: e.tensor_scalar(kg[0:C, :], ktf[0:C, :], gcc[0:C, 16:17], None, ALU.mult), [ktfb, B_gcc], [kgb])
            p7, p7b = PS.next()
            mm(p7[0:C, 0:128], vc, ident_b[:, :], [B_vT, B_ident], [p7b])
            vb_, vbb_ = UB["vb"]
            S.op("dve", lambda e: e.tensor_scalar(vb_[0:C, :], p7[0:C, 0:128], b_col, None, ALU.mult), [p7b, gbuf], [vbb_])
            p8, p8b = PS.next()
            mm(p8[0:C, 0:128], tt[0:C, 0:C], vb_[0:C, :], [ttb, vbb_], [p8b])
            uf, ufb = UF["uf"]
            evac(uf[0:C, :], p8[0:C, 0:128], [p8b], [ufb])
            p9, p9b = PS.next()
            mm(p9[:, 0:C], kbg[0:C, :], tt[0:C, 0:C], [kbgb, ttb], [p9b])
            wT, wTb = UB["wT"]
            evac(wT[:, 0:C], p9[:, 0:C], [p9b], [wTb])
            qg, qgb = UB["qgT"]
            S.op("act", lambda e: e.activation(UF["tB"][0][:, 0:C], UF["grow"][0][:, 0:C], AF.Exp), [fb("grow")], [fb("tB")])
            S.op("dve", lambda e: e.tensor_tensor(qg[:, 0:C], qc, UF["tB"][0][:, 0:C], ALU.mult), [B_qh, fb("tB")], [qgb])
            pv, pvb = PS.next()
            mm(pv[0:C, 0:NX], wT[:, 0:C], Xb[:, h, 0:NX], [wTb, B_Xb], [pvb])
            S.op("dve", lambda e: e.tensor_tensor(vnx[0:C, 0:128], uf[0:C, :], pv[0:C, 0:128], ALU.subtract), [ufb, pvb], [B_vnx])
            if ext:
                S.op("dve", lambda e: e.tensor_scalar(vnx[0:C, 128:256], pv[0:C, 128:256], -1.0, None, ALU.mult),
                     [pvb, B_vnx], [B_vnx])
            po_, pob_ = PS.next()
            mm(po_[:, 0:C], Xb[:, h, 0:128], qg[:, 0:C], [B_Xb, qgb], [pob_], start=True, stop=False)
            mm(po_[:, 0:C], vnx[0:C, 0:128], aq[0:C, 0:C], [B_vnx, aqb], [pob_], start=False, stop=True)
            os_, osb_ = ost_r.next()
            evac(os_[:, 0:C], po_[:, 0:C], [pob_], [osb_])
            S.dma("sp", ol_dst, os_[:, 0:C], reads=[osb_], writes=[B_OL])
            if ext:
                pp_, ppb_ = PS.next()
                mm(pp_[:, 0:C], Xb[:, h, 128:256], qg[:, 0:C], [B_Xb, qgb], [ppb_], start=True, stop=False)
                mm(pp_[:, 0:C], vnx[0:C, 128:256], aq[0:C, 0:C], [B_vnx, aqb], [ppb_], start=False, stop=True)
                ob_, obb_ = osb_r.next()
                evac(ob_[:, 0:C], pp_[:, 0:C], [ppb_], [obb_])
                S.dma("sp", op_dst, ob_[:, 0:C], reads=[obb_], writes=[B_OP])
            px_, pxb_ = PS.next()
            mm(px_[:, 0:NX], kg[0:C, :], vnx[0:C, 0:NX], [kgb, B_vnx], [pxb_])
            S.op("dve", lambda e: e.scalar_tensor_tensor(out=Xf[:, h, 0:NX], in0=Xf[:, h, 0:NX], scalar=gcc[:, 17:18],
                                                         in1=px_[:, 0:NX], op0=ALU.mult, op1=ALU.add),
                 [pxb_, B_Xf, B_gcc], [B_Xf])
            S.op("act", lambda e: e.copy(Xb[:, h, 0:NX], Xf[:, h, 0:NX]), [B_Xf], [B_Xb])

        erow = UF["tB"][0]

        def gates(h_, hb_, r0, nrows, tile_i):
            wv, wb = load_w512(gwin, 4096, 16)
            p, pb = tokmajor(h_, hb_, r0, nrows, wv, wb, 16)
            S.op("dve", lambda e: e.tensor_copy(Gt[0:nrows, tile_i, :], p[0:nrows, 0:16]), [pb], [B_Gt])
            S.op("dve", lambda e: e.tensor_tensor(Gt[0:nrows, tile_i, 0:8], Gt[0:nrows, tile_i, 0:8], gsm[0:nrows, 0:8], ALU.add),
                 [B_Gt, B_gsm], [B_Gt])
            S.op("act", lambda e: e.activation(Gt[0:nrows, tile_i, 0:8], Gt[0:nrows, tile_i, 0:8], AF.Exp), [B_Gt], [B_Gt])
            S.op("act", lambda e: e.activation(Gt[0:nrows, tile_i, 0:8], Gt[0:nrows, tile_i, 0:8], AF.Ln, bias=1.0), [B_Gt], [B_Gt])
            S.op("dve", lambda e: e.tensor_tensor(Gt[0:nrows, tile_i, 0:8], Gt[0:nrows, tile_i, 0:8], gsm[0:nrows, 8:16], ALU.mult),
                 [B_Gt, B_gsm], [B_Gt])
            S.op("act", lambda e: e.activation(Gt[0:nrows, tile_i, 8:16], Gt[0:nrows, tile_i, 8:16], AF.Sigmoid), [B_Gt], [B_Gt])

        def chunk_cols(C, tile_i, r_lo, bcols=None, bbuf=None):
            p, pb = PS.next()
            if C == 128:
                mm(p[0:C, 0:8], triu_f[:, :], Gt[:, tile_i, 0:8], [B_ident, B_Gt], [pb])
            else:
                mm(p[0:C, 0:8], selg[0:64, r_lo // 16, :], Gt[0:64, tile_i, 0:8], [B_selg, B_Gt], [pb])
            S.op("dve", lambda e: e.tensor_copy(gcc[0:C, 0:8], p[0:C, 0:8]), [pb], [B_gcc])
            S.op("act", lambda e: e.activation(gcc[0:C, 8:16], gcc[0:C, 0:8], AF.Exp), [B_gcc], [B_gcc])
            if bcols is None:
                bcols, bbuf = Gt[0:C, tile_i, 8:16], B_Gt
            S.op("dve", lambda e: e.tensor_tensor(gcc[0:C, 8:16], gcc[0:C, 8:16], bcols, ALU.mult), [B_gcc, bbuf], [B_gcc])

        def conv_tile(h_, hb_, T, tok0, halo_loader, zcol0):
            for ct in range(32):
                wv, wb = load_w128(gwin, ct * 128)
                p, pb = featmajor(h_, hb_, T, wv, wb)
                if ct >= 24:
                    os_, osb_ = ac_r.next()
                    evac(os_[:, 0:T], p[:, 0:T], [pb], [osb_])
                    S.dma("sp", Z_d.ap()[ct - 24, :, zcol0:zcol0 + T], os_[:, 0:T], reads=[osb_], writes=[B_Z])
                    continue
                xp, xpb = xp_r.next()
                evac(xp[:, 3:3 + T], p[:, 0:T], [pb], [xpb])
                halo_loader(ct, xp, xpb)
                ac, acb = ac_r.next()
                S.op("dve", lambda e, ac=ac, xp=xp, ct=ct: e.tensor_scalar(ac[:, 0:T], xp[:, 0:T], cw[:, ct, 0:1], None, ALU.mult),
                     [xpb, B_cw], [acb])
                for j in range(1, 4):
                    S.op("dve", lambda e, ac=ac, xp=xp, ct=ct, j=j: e.scalar_tensor_tensor(
                        out=ac[:, 0:T], in0=xp[:, j:j + T], scalar=cw[:, ct, j:j + 1], in1=ac[:, 0:T], op0=ALU.mult,
                        op1=ALU.add), [xpb, B_cw, acb], [acb])
                S.op("dve", lambda e, xp=xp, ct=ct: e.tensor_copy(carry[:, ct, :], xp[:, T:T + 3]), [xpb], [B_cy])
                S.op("act", lambda e, ac=ac: e.activation(ac[:, 0:T], ac[:, 0:T], AF.Silu), [acb], [acb])
                hh = ct % 8
                if ct >= 16:
                    S.op("dve", lambda e, ac=ac, hh=hh: e.tensor_copy(vT[:, hh, 0:T], ac[:, 0:T]), [acb], [B_vT])
                    continue
                sq2, sq2b = xp_r.next()
                S.op("act", lambda e, sq2=sq2, ac=ac: e.activation(sq2[:, 0:T], ac[:, 0:T], AF.Square), [acb], [sq2b])
                ps_, psb_ = PS.next()
                mm(ps_[:, 0:T], ones_f[:, :], sq2[:, 0:T], [sq2b, B_ones], [psb_])
                S.op("act", lambda e, sq2=sq2, ps_=ps_: e.activation(sq2[:, 0:T], ps_[:, 0:T], AF.Ln, bias=EPS), [psb_], [sq2b])
                S.op("act", lambda e, sq2=sq2: e.activation(sq2[:, 0:T], sq2[:, 0:T], AF.Exp, scale=-0.5), [sq2b], [sq2b])
                if ct < 8:
                    S.op("dve", lambda e, ac=ac, sq2=sq2, hh=hh: e.scalar_tensor_tensor(
                        out=qhT[:, hh, 0:T], in0=ac[:, 0:T], scalar=128 ** -0.5, in1=sq2[:, 0:T], op0=ALU.mult, op1=ALU.mult),
                        [acb, sq2b], [B_qh])
                else:
                    S.op("dve", lambda e, ac=ac, sq2=sq2, hh=hh: e.tensor_tensor(khT[:, hh, 0:T], ac[:, 0:T], sq2[:, 0:T], ALU.mult),
                         [acb, sq2b], [B_kh])

        halo0, B_h0 = AFa.alloc([24, 3], "halo0")
        hg, B_hg = AFa.alloc([24, 24], "hg")
        if 'ag' not in SKIP:
            S.op("pool", lambda e: e.collective_compute("AllGather", ALU.bypass, replica_groups=[list(range(NCORES))],
                                                        ins=[GCs.ap().opt()], outs=[GCr.ap().opt()]),
                 [B_GCs], [B_GCr], dma=True, inc=1)
            for rr in range(24):
                S.dma("sp", hg[:, :, rr], GCr.ap()[rr].rearrange("(ct p) -> p ct", p=128), reads=[B_GCr], writes=[B_hg],
                      allow_slow_non_contiguous=True)
            hgv = hg[:].rearrange("p ct (r j) -> p ct r j", j=3)
            S.op("dve", lambda e: e.tensor_scalar(halo0[:], hgv[:, :, 0, :], meta_sb[:, 17:18], None, ALU.mult), [B_hg, B_meta], [B_h0])
            for r in range(1, 8):
                S.op("dve", lambda e, r=r: e.scalar_tensor_tensor(out=halo0[:], in0=hgv[:, :, r, :], scalar=meta_sb[:, 17 + r:18 + r],
                                                                   in1=halo0[:], op0=ALU.mult, op1=ALU.add), [B_hg, B_meta, B_h0], [B_h0])
        else:
            S.op("pool", lambda e: e.memset(halo0[:], 0.0), [], [B_h0])
        S.op("dve", lambda e: e.tensor_copy(carry[:], halo0[:]), [B_h0], [B_cy])
        S.op("pool", lambda e: e.memset(Xf[:], 0.0), [], [B_Xf])
        for h in range(8):
            S.op("dve", lambda e, h=h: e.tensor_copy(Xf[:, h, 128:256], ident[:, :]), [B_ident, B_Xf], [B_Xf])
        S.op("act", lambda e: e.copy(Xb[:].rearrange("p h x -> p (h x)"), Xf[:].rearrange("p h x -> p (h x)")), [B_Xf], [B_Xb])

        def halo_from_carry(ct, xp, xpb):
            S.op("dve", lambda e: e.tensor_copy(xp[:, 0:3], carry[:, ct, :]), [B_cy, xpb], [xpb])

        for ti in range(4):
            c0 = ti * 512
            h_, hb_ = rmsnorm(xT, c0, 512, [B_x[ti]], 5)
            for j in range(4):
                gates(h_, hb_, j * 128, 128, ti * 4 + j)
            conv_tile(h_, hb_, 512, c0, halo_from_carry, c0)
            for j in range(4):
                chunk_cols(128, ti * 4 + j, 0)
                for h in range(8):
                    unit(128, h, j * 128, Gt[:, ti * 4 + j, h:h + 1], Gt[:, ti * 4 + j, 8 + h:9 + h], True,
                         OL_d.ap()[h, :, c0 + j * 128:c0 + (j + 1) * 128], OP_d.ap()[h, :, c0 + j * 128:c0 + (j + 1) * 128])

        for ch in range(4):
            for hh in range(2):
                S.dma("sp", XS[ch].ap()[hh * 128:(hh + 1) * 128, :], Xf[:, ch * 2 + hh, :], reads=[B_Xf], writes=[B_XS[ch]])
        Scur = AFa.t[:, uf_off:uf_off + 1024].rearrange("p (a b) -> p a b", a=8)
        B_Scur = Buf("Scur")
        S.op("pool", lambda e: e.memset(Sst[:], 0.0), [], [B_Sst])
        S.op("pool", lambda e: e.memset(Scur[:], 0.0), ALLUF, [B_Scur] + ALLUF)
        if 'ag' not in SKIP:
            xg_r = xp_r
            for ch in range(4):
                S.op("pool", lambda e, ch=ch: e.collective_compute("AllGather", ALU.bypass, replica_groups=[list(range(NCORES))],
                                                                   ins=[XS[ch].ap().opt()], outs=[XR[ch].ap().opt()]),
                     [B_XS[ch]], [B_XR[ch]], dma=True, inc=1)
            for r in range(8):
                for h in range(8):
                    xg, xgb = xg_r.next()
                    S.dma("sp", xg[:, 0:256], XR[h // 2].ap()[r * 256 + (h % 2) * 128:r * 256 + (h % 2) * 128 + 128, :],
                          reads=[B_XR[h // 2]], writes=[xgb])
                    pt_, ptb_ = PS.next()
                    S.op("pe", lambda e, pt_=pt_, xg=xg: e.transpose(pt_[:, 0:128], xg[:, 128:256], ident[:, :]), [xgb, B_ident], [ptb_])
                    gt_, gtb_ = ac_r.next()
                    evac(gt_[:, 0:128], pt_[:, 0:128], [ptb_], [gtb_])
                    pn_, pnb_ = PS.next()
                    mm(pn_[:, 0:128], gt_[:, 0:128], Scur[:, h, :], [gtb_, B_Scur], [pnb_])
                    S.op("dve", lambda e, pn_=pn_, xg=xg, h=h: e.tensor_tensor(Scur[:, h, :], pn_[:, 0:128], xg[:, 0:128], ALU.add),
                         [pnb_, xgb, B_Scur], [B_Scur])
                    if r < 7:
                        S.op("dve", lambda e, h=h, r=r: e.scalar_tensor_tensor(
                            out=Sst[:, h, :], in0=Scur[:, h, :], scalar=meta_sb[:, 25 + r:26 + r], in1=Sst[:, h, :],
                            op0=ALU.mult, op1=ALU.add), [B_Scur, B_meta, B_Sst], [B_Sst])
        else:
            for h in range(8):
                S.op("dve", lambda e, h=h: e.tensor_copy(Scur[:, h, :], Xf[:, h, 0:128]), [B_Xf, B_Scur], [B_Scur])
        gsp_st, B_gspst = AFa.t[:, uf_off + 1024:uf_off + 2048].rearrange("p (a b) -> p a b", a=8), Buf("gspst")
        S.op("dve", lambda e: e.tensor_copy(gsp_st, Scur), [B_Scur] + ALLUF, [B_gspst] + ALLUF)
        for h in range(8):
            S.dma("sp", o_gsp.ap()[h], gsp_st[:, h, :], reads=[B_gspst] + ALLUF)

        def sample_part():
            h_, hb_ = rmsnorm(xT, TP, TS, [B_x[4]], 5)
            gates(h_, hb_, 0, TS, 16)
            shalo, B_sh = AFa.alloc([24, 12], "shalo")
            for bj in range(12):
                S.dma("sp", shalo[:, :, bj], st_conv.ap()[bj // 3, bj % 3].rearrange("(ct p) -> p ct", p=128), writes=[B_sh],
                      allow_slow_non_contiguous=True)
            for ct in range(32):
                wv, wb = load_w128(gwin, ct * 128)
                p, pb = featmajor(h_, hb_, TS, wv, wb)
                if ct >= 24:
                    os_, osb_ = ac_r.next()
                    evac(os_[:, 0:TS], p[:, 0:TS], [pb], [osb_])
                    S.dma("sp", Z_d.ap()[ct - 24, :, TP:TP + TS], os_[:, 0:TS], reads=[osb_], writes=[B_Z])
                    continue
                xp, xpb = xp_r.next()
                xpv = xp[:, 0:76].rearrange("p (b c) -> p b c", b=4)
                evac(xpv[:, :, 3:19], p[:, 0:TS].rearrange("p (b c) -> p b c", b=4), [pb], [xpb])
                S.op("dve", lambda e, xpv=xpv, ct=ct: e.tensor_copy(xpv[:, :, 0:3], shalo[:, ct, :].rearrange("p (b j) -> p b j", b=4)),
                     [B_sh, xpb], [xpb])
                ac, acb = ac_r.next()
                acv = ac[:, 0:TS].rearrange("p (b c) -> p b c", b=4)
                S.op("dve", lambda e, acv=acv, xpv=xpv, ct=ct: e.tensor_scalar(acv, xpv[:, :, 0:16], cw[:, ct, 0:1], None, ALU.mult),
                     [xpb, B_cw], [acb])
                for j in range(1, 4):
                    S.op("dve", lambda e, acv=acv, xpv=xpv, ct=ct, j=j: e.scalar_tensor_tensor(
                        out=acv, in0=xpv[:, :, j:j + 16], scalar=cw[:, ct, j:j + 1], in1=acv, op0=ALU.mult, op1=ALU.add),
                        [xpb, B_cw, acb], [acb])
                S.op("act", lambda e, ac=ac: e.activation(ac[:, 0:TS], ac[:, 0:TS], AF.Silu), [acb], [acb])
                hh = ct % 8
                if ct >= 16:
                    S.op("dve", lambda e, ac=ac, hh=hh: e.tensor_copy(vT[:, hh, 0:TS], ac[:, 0:TS]), [acb], [B_vT])
                    continue
                sq2, sq2b = xp_r.next()
                S.op("act", lambda e, sq2=sq2, ac=ac: e.activation(sq2[:, 0:TS], ac[:, 0:TS], AF.Square), [acb], [sq2b])
                ps_, psb_ = PS.next()
                mm(ps_[:, 0:TS], ones_f[:, :], sq2[:, 0:TS], [sq2b, B_ones], [psb_])
                S.op("act", lambda e, sq2=sq2, ps_=ps_: e.activation(sq2[:, 0:TS], ps_[:, 0:TS], AF.Ln, bias=EPS), [psb_], [sq2b])
                S.op("act", lambda e, sq2=sq2: e.activation(sq2[:, 0:TS], sq2[:, 0:TS], AF.Exp, scale=-0.5), [sq2b], [sq2b])
                if ct < 8:
                    S.op("dve", lambda e, ac=ac, sq2=sq2, hh=hh: e.scalar_tensor_tensor(
                        out=qhT[:, hh, 0:TS], in0=ac[:, 0:TS], scalar=128 ** -0.5, in1=sq2[:, 0:TS], op0=ALU.mult, op1=ALU.mult),
                        [acb, sq2b], [B_qh])
                else:
                    S.op("dve", lambda e, ac=ac, sq2=sq2, hh=hh: e.tensor_tensor(khT[:, hh, 0:TS], ac[:, 0:TS], sq2[:, 0:TS], ALU.mult),
                         [acb, sq2b], [B_kh])
            for bb in range(4):
                for h in range(8):
                    S.dma("sp", Xf[:, h, 0:128], st_gdn.ap()[bb, h], writes=[B_Xf])
                S.op("act", lambda e: e.copy(Xb[:].rearrange("p h x -> p (h x)"), Xf[:].rearrange("p h x -> p (h x)")), [B_Xf], [B_Xb])
                pg_, pgb_ = PS.next()
                mm(pg_[0:16, 0:16], selx[0:64, bb, :], Gt[0:64, 16, :], [B_selg, B_Gt], [pgb_])
                S.op("dve", lambda e, pg_=pg_: e.tensor_copy(gcc[0:16, 18:24], pg_[0:16, 0:6]) if False else
                     e.tensor_copy(gsb[0:16, :], pg_[0:16, 0:16]), [pgb_], [B_gsb])
                chunk_cols(16, 16, bb * 16, gsb[0:16, 8:16], B_gsb)
                for h in range(8):
                    unit(16, h, bb * 16, gsb[0:16, h:h + 1], gsb[0:16, 8 + h:9 + h], False,
                         OL_d.ap()[h, :, TP + bb * 16:TP + (bb + 1) * 16], None, gbuf=B_gsb)
                for h in range(8):
                    S.dma("sp", o_gss.ap()[bb, h], Xf[:, h, 0:128], reads=[B_Xf])

        selg, B_selg = AFa.alloc([4, 16], "selg")
        selx, B_selx0 = AFa.alloc([4, 16], "selx")
        gsb, B_gsb = AFa.alloc([16], "gsb")
        for bb in range(4):
            S.op("pool", lambda e, bb=bb: e.memset(selg[0:64, bb, :], 1.0), [], [B_selg])
            S.op("pool", lambda e, bb=bb: e.affine_select(out=selg[0:64, bb, :], in_=selg[0:64, bb, :], compare_op=ALU.is_ge,
                                                          fill=0.0, base=16 * bb, pattern=[[1, 16]], channel_multiplier=-1),
                 [B_selg], [B_selg])
            S.op("pool", lambda e, bb=bb: e.affine_select(out=selg[0:64, bb, :], in_=selg[0:64, bb, :], compare_op=ALU.is_ge,
                                                          fill=0.0, base=-16 * bb, pattern=[[0, 16]], channel_multiplier=1),
                 [B_selg], [B_selg])
            S.op("pool", lambda e, bb=bb: e.memset(selx[0:64, bb, :], 1.0), [], [B_selg])
            S.op("pool", lambda e, bb=bb: e.affine_select(out=selx[0:64, bb, :], in_=selx[0:64, bb, :], compare_op=ALU.is_ge,
                                                          fill=0.0, base=16 * bb, pattern=[[1, 16]], channel_multiplier=-1),
                 [B_selg], [B_selg])
            S.op("pool", lambda e, bb=bb: e.affine_select(out=selx[0:64, bb, :], in_=selx[0:64, bb, :], compare_op=ALU.is_ge,
                                                          fill=0.0, base=-16 * bb, pattern=[[-1, 16]], channel_multiplier=1),
                 [B_selg], [B_selg])
        if 'gdns' not in SKIP:
            sample_part()

        old_gsm, old_Sst = B_gsm, B_Sst
        AB.reset()
        AFa.reset()
        gsm, B_gsm = AFa.alloc([32], "gsm2", inherit=old_gsm)
        Sst, B_Sst = AFa.alloc([8, 128], "Sst2", inherit=old_Sst)
        ac_r = Ring(S, "acg2", 2, [512], F32, arena=AFa)
        Sstb, B_Sstb = AB.alloc([8, 128], "Sstb")
        S.op("act", lambda e: e.copy(Sstb[:].rearrange("p h x -> p (h x)"), Sst[:].rearrange("p h x -> p (h x)")), [B_Sst], [B_Sstb])
        olf_r = Ring(S, "olf", 2, [512], F32, arena=AFa)
        opb_r = Ring(S, "opb", 2, [512], BF16, arena=AB)
        zf_r = Ring(S, "zf", 2, [512], F32, arena=AFa)
        gob_r = Ring(S, "gob", 2, [512], BF16, arena=AB)
        for (c0, T, xb_) in TILES:
            for h in range(8):
                ol, olb = olf_r.next()
                S.dma("sp", ol[:, 0:T], OL_d.ap()[h, :, c0:c0 + T], reads=[B_OL], writes=[olb])
                zf, zfb = zf_r.next()
                S.dma("sp", zf[:, 0:T], Z_d.ap()[h, :, c0:c0 + T], reads=[B_Z], writes=[zfb])
                if c0 < TP:
                    opb, opbb = opb_r.next()
                    S.dma("sp", opb[:, 0:T], OP_d.ap()[h, :, c0:c0 + T], reads=[B_OP], writes=[opbb])
                    pc_, pcb_ = PS.next()
                    mm(pc_[:, 0:T], Sstb[:, h, :], opb[:, 0:T], [B_Sstb, opbb], [pcb_])
                    S.op("dve", lambda e, ol=ol, pc_=pc_, T=T: e.tensor_tensor(ol[:, 0:T], ol[:, 0:T], pc_[:, 0:T], ALU.add),
                         [pcb_, olb], [olb])
                sq2, sq2b = ac_r.next()
                S.op("act", lambda e, sq2=sq2, ol=ol, T=T: e.activation(sq2[:, 0:T], ol[:, 0:T], AF.Square), [olb], [sq2b])
                ps_, psb_ = PS.next()
                mm(ps_[:, 0:T], ones_f[:, :], sq2[:, 0:T], [sq2b, B_ones], [psb_])
                S.op("act", lambda e, sq2=sq2, ps_=ps_, T=T: e.activation(sq2[:, 0:T], ps_[:, 0:T], AF.Ln, bias=EPS, scale=1.0 / 128),
                     [psb_], [sq2b])
                S.op("act", lambda e, sq2=sq2, T=T: e.activation(sq2[:, 0:T], sq2[:, 0:T], AF.Exp, scale=-0.5), [sq2b], [sq2b])
                S.op("dve", lambda e, ol=ol, sq2=sq2, T=T: e.scalar_tensor_tensor(
                    out=ol[:, 0:T], in0=ol[:, 0:T], scalar=gsm[:, 16:17], in1=sq2[:, 0:T], op0=ALU.mult, op1=ALU.mult),
                    [olb, sq2b, B_gsm], [olb])
                S.op("act", lambda e, zf=zf, T=T: e.activation(zf[:, 0:T], zf[:, 0:T], AF.Silu), [zfb], [zfb])
                gob, gobb = gob_r.next()
                S.op("dve", lambda e, gob=gob, ol=ol, zf=zf, T=T: e.tensor_tensor(gob[:, 0:T], ol[:, 0:T], zf[:, 0:T], ALU.mult),
                     [olb, zfb], [gobb])
                S.dma("sp", G_d.ap()[h, :, c0:c0 + T], gob[:, 0:T], reads=[gobb], writes=[B_G])

    if stage_end >= 4:
        gdn_phase()
        out_proj(gdn_w_o.ap()[0], G_d, B_G)
    if stage_end >= 5:
        cross_attn(1)
        out_proj(xa_wo.ap()[1], X_d, B_X)
        alloc_ffn()
        for (c0, T, xb) in TILES:
            ffn(1, 1, 7, xT, c0, T, [xb])
        for (c0, T, xb) in TILES:
            q, qb = R["sq"].next()
            S.op("act", lambda e, q=q, c0=c0, T=T: e.activation(q[:, :, 0:T], xT[:, :, c0:c0 + T], AF.Square), [xb], [qb])
            p, pb = PS.next()
            for kt in range(KT):
                S.op("pe", lambda e, kt=kt, p=p, q=q, T=T: e.matmul(p[:, 0:T], ones_f[:], q[:, kt, 0:T], start=(kt == 0),
                                                                    stop=(kt == KT - 1)), [qb, B_ones], [pb])
            r, rb = R["rstd"].next()
            S.op("act", lambda e, r=r, p=p, T=T: e.activation(r[:, 0:T], p[:, 0:T], AF.Ln, bias=EPS, scale=1.0 / D), [pb], [rb])
            S.op("act", lambda e, r=r, T=T: e.activation(r[:, 0:T], r[:, 0:T], AF.Exp, scale=-0.5), [rb], [rb])
            for kt in range(KT):
                S.op("dve", lambda e, kt=kt, r=r, c0=c0, T=T: e.scalar_tensor_tensor(
                    out=xT[:, kt, c0:c0 + T], in0=xT[:, kt, c0:c0 + T], scalar=gcol[:, 10, kt:kt + 1], in1=r[:, 0:T],
                    op0=ALU.mult, op1=ALU.mult), [xb, rb, B_g], [xb])

    alloc_ffn()
    for i in range(TP // 128):
        store_transposed(xT, i * 128, 128, o_yp.ap()[i * 128:(i + 1) * 128, :], [B_x[i // 4]])
    store_transposed(xT, TP, TS, o_ys.ap()[0:TS, :], [B_x[4]])
    S.emit()
    return nc


def kernel(**inp):
    f = lambda a: np.ascontiguousarray(np.asarray(a, dtype=np.float32))
    xp = f(inp["x_prompt"])[0]
    xs = f(inp["x_sample"]).reshape(32 * 16, D)
    xpad = np.concatenate([np.zeros((HALO, D), np.float32), xp], axis=0)
    shared = {
        "mem": f(inp["mem_prompt"])[0],
        "norm_g": f(inp["norm_g"]), "mem_norm_g": f(inp["mem_norm_g"]), "final_norm_g": f(inp["final_norm_g"]),
        "ffn_w_gate": f(inp["ffn_w_gate"]) if not NOFFN else np.zeros((1, 1, 8, 8), np.float32),
        "ffn_w_up": f(inp["ffn_w_up"]) if not NOFFN else np.zeros((1, 1, 8, 8), np.float32),
        "ffn_w_down": f(inp["ffn_w_down"]) if not NOFFN else np.zeros((1, 1, 8, 8), np.float32),
        "xa_w_q": f(inp["xa_w_q"]), "xa_w_k": f(inp["xa_w_k"]), "xa_w_v": f(inp["xa_w_v"]), "xa_w_o": f(inp["xa_w_o"]),
        "ab_w_in": f(inp["ab_w_in"]), "ab_b_f": f(inp["ab_b_f"]), "ab_rel_bias": f(inp["ab_rel_bias"]),
        "ab_w_o": f(inp["ab_w_o"]), "gdn_w_in": f(inp["gdn_w_in"]),
        "gdn_conv_w": f(inp["gdn_conv_w"]), "gdn_a_log": f(inp["gdn_a_log"]), "gdn_dt_bias": f(inp["gdn_dt_bias"]),
        "gdn_norm_g": f(inp["gdn_norm_g"]), "gdn_w_o": f(inp["gdn_w_o"]),
    }
    cak = f(inp["cache_a_k"])[0].reshape(32, 512, 512)
    cav = f(inp["cache_a_v"])[0].reshape(32, 512, 512)
    cbk = f(inp["cache_b_k"])[0].reshape(32, 1024, 512)
    cbv = f(inp["cache_b_v"])[0].reshape(32, 1024, 512)
    cbl = f(inp["cache_b_logf"])[0]
    cmk = f(inp["cache_mem_k"]).reshape(2, 32, 256, D)
    cmv = f(inp["cache_mem_v"]).reshape(2, 32, 256, D)
    in_maps = []
    for c in range(NCORES):
        m = dict(shared)
        m["xp"] = np.ascontiguousarray(xpad[c * TP:c * TP + HALO + TP])
        m["xs"] = np.ascontiguousarray(xs[c * TS:(c + 1) * TS])
        meta = np.zeros((128, 32), np.float32)
        meta[:, 0] = 1.0 if c > 0 else 0.0
        for r in range(8):
            meta[:, 1 + r] = 1.0 if r < c else 0.0
            meta[:, 9 + r] = 0.0 if r < c else -30000.0
            meta[:, 17 + r] = 1.0 if r == c - 1 else 0.0
            if r < 7:
                meta[:, 25 + r] = 1.0 if r + 1 == c else 0.0
        m["meta"] = meta
        sl = slice(4 * c, 4 * c + 4)
        m["c_ak"], m["c_av"], m["c_bk"], m["c_bv"], m["c_bl"] = (np.ascontiguousarray(a[sl]) for a in (cak, cav, cbk, cbv, cbl))
        m["c_mk"] = np.ascontiguousarray(cmk[:, sl])
        m["st_gdn"] = np.ascontiguousarray(f(inp["state_gdn"])[0, sl])
        m["st_conv"] = np.ascontiguousarray(f(inp["state_gdn_conv"])[0, sl])
        m["c_mv"] = np.ascontiguousarray(cmv[:, sl])
        in_maps.append(m)
    nc = build()
    res = run_bass_kernel_spmd(nc, in_maps, core_ids=list(range(NCORES)))
    r = res.results
    kernel.last = r
    cat = lambda k: np.concatenate([r[c][k] for c in range(NCORES)], axis=0)
    y_p = cat("o_yp").reshape(1, 16384, D)
    y_s = cat("o_ys").reshape(32, 16, D)
    z = lambda *s: np.zeros(s, np.float32)
    outs = (y_p, y_s,
            r[7]["o_akp"].reshape(1, 1, 512, 8, 64), r[7]["o_avp"].reshape(1, 1, 512, 8, 64),
            cat("o_bkp").reshape(1, 1, 16384, 8, 64), cat("o_bvp").reshape(1, 1, 16384, 8, 64),
            cat("o_blp").reshape(1, 1, 16384, 8),
            r[0]["o_gsp"].reshape(1, 1, 8, 128, 128), r[7]["o_gcp"].reshape(1, 1, 3, 3072),
            r[0]["o_mkp"].reshape(2, 1, 256, 4, 256), r[0]["o_mvp"].reshape(2, 1, 256, 4, 256),
            cat("o_aks").reshape(1, 32, 16, 8, 64), cat("o_avs").reshape(1, 32, 16, 8, 64),
            cat("o_bks").reshape(1, 32, 16, 8, 64), cat("o_bvs").reshape(1, 32, 16, 8, 64),
            cat("o_bls").reshape(1, 32, 16, 8),
            cat("o_gss").reshape(1, 32, 8, 128, 128), cat("o_gcs").reshape(1, 32, 3, 3072))
    return outs
```
